# Optimizing a Trainium2 kernel written in Bass

```python
import jax, jax.numpy as jnp
from jax import lax
import numpy as np


D_MODEL = 1024
BATCH = 16
SEQ = 4096
DEPTH = 4
DEC_BATCH = 16
DEC_SEQ = 2048
PAST_LEN = 128

N_MIXERS = 3
GRID_W = 64
EPS = 1e-6
FNET_GROUPS = 4
FNET_GROUP_DIM = D_MODEL // FNET_GROUPS
HEAD_DIM = 128
N_HEADS = D_MODEL // HEAD_DIM
N_KV_HEADS = 2
Q_PER_KV = N_HEADS // N_KV_HEADS
QKV_DIM = (N_HEADS + 2 * N_KV_HEADS) * HEAD_DIM
AXIS_ROT_DIM = HEAD_DIM // 2
ROPE_THETA = 10000.0
Q_BLOCK = 128
SGU_DIM = D_MODEL
SGU_GROUPS = 8
SGU_GROUP_DIM = SGU_DIM // SGU_GROUPS
SGU_CHUNK = 128
FFN_HIDDEN = -(-8 * D_MODEL // (3 * 256)) * 256

kernel_name = 'hybrid_fnet_gqa_gmlp_adaln_encoder'


def rms_norm(x, g):
    xf = x.astype(jnp.float32)
    y = xf * lax.rsqrt(jnp.mean(xf * xf, axis=-1, keepdims=True) + EPS)
    return (y * g.astype(jnp.float32)).astype(x.dtype)


def layer_norm(x, g, b):
    xf = x.astype(jnp.float32)
    mu = jnp.mean(xf, axis=-1, keepdims=True)
    xc = xf - mu
    y = xc * lax.rsqrt(jnp.mean(xc * xc, axis=-1, keepdims=True) + EPS)
    return (y * g.astype(jnp.float32) + b.astype(jnp.float32)).astype(x.dtype)


def modulate(h, shift, scale):
    return h * (1.0 + scale[:, None, :]) + shift[:, None, :]


def fourier_mixer(h, w_o):
    B, S, _ = h.shape
    hg = h.astype(jnp.float32).reshape(B, S, FNET_GROUPS, FNET_GROUP_DIM)
    f = jnp.fft.fftn(hg, axes=(1, 3), norm='ortho').real
    return f.reshape(B, S, D_MODEL).astype(h.dtype) @ w_o


def axial_rope_tables(S):
    rows = S // GRID_W
    row = jnp.repeat(jnp.arange(rows), GRID_W).astype(jnp.float32)
    col = jnp.tile(jnp.arange(GRID_W), rows).astype(jnp.float32)
    freqs = 1.0 / (ROPE_THETA ** (jnp.arange(0, AXIS_ROT_DIM, 2, dtype=jnp.float32) / AXIS_ROT_DIM))
    ang_r = row[:, None] * freqs[None, :]
    ang_c = col[:, None] * freqs[None, :]
    return jnp.cos(ang_r), jnp.sin(ang_r), jnp.cos(ang_c), jnp.sin(ang_c)


def rotate_half_rope(x, cos, sin):
    x1, x2 = jnp.split(x, 2, axis=-1)
    c = cos[:, None, :].astype(x.dtype)
    s = sin[:, None, :].astype(x.dtype)
    return jnp.concatenate([x1 * c - x2 * s, x2 * c + x1 * s], axis=-1)


def apply_axial_rope(x, tabs):
    cos_r, sin_r, cos_c, sin_c = tabs
    xr, xc = jnp.split(x, 2, axis=-1)
    return jnp.concatenate([rotate_half_rope(xr, cos_r, sin_r), rotate_half_rope(xc, cos_c, sin_c)], axis=-1)


def attention_mixer(h, w_qkv, q_g, k_g, w_o):
    B, S, _ = h.shape
    qkv = h @ w_qkv
    q, k, v = jnp.split(qkv, [N_HEADS * HEAD_DIM, (N_HEADS + N_KV_HEADS) * HEAD_DIM], axis=-1)
    q = rms_norm(q.reshape(B, S, N_HEADS, HEAD_DIM), q_g)
    k = rms_norm(k.reshape(B, S, N_KV_HEADS, HEAD_DIM), k_g)
    v = v.reshape(B, S, N_KV_HEADS, HEAD_DIM)
    tabs = axial_rope_tables(S)
    q = apply_axial_rope(q, tabs) * (HEAD_DIM ** -0.5)
    k = apply_axial_rope(k, tabs)
    nb = S // Q_BLOCK
    qb = q.reshape(B, nb, Q_BLOCK, N_KV_HEADS, Q_PER_KV, HEAD_DIM).transpose(1, 0, 2, 3, 4, 5)

    def block(qi):
        s = jnp.einsum('bqkgd,bskd->bkgqs', qi, k).astype(jnp.float32)
        p = jax.nn.softmax(s, axis=-1).astype(v.dtype)
        return jnp.einsum('bkgqs,bskd->bqkgd', p, v)

    o = lax.map(block, qb)
    o = o.transpose(1, 0, 2, 3, 4, 5).reshape(B, S, N_HEADS * HEAD_DIM)
    return o @ w_o


def spatial_gating_mixer(h, w_in, ln_g, ln_b, w_s, b_s, w_o):
    B, S, _ = h.shape
    uv = jax.nn.gelu(h @ w_in, approximate=False)
    u, v = jnp.split(uv, 2, axis=-1)
    v = layer_norm(v, ln_g, ln_b)
    nc = S // SGU_CHUNK
    vc = v.reshape(B, nc, SGU_CHUNK, SGU_GROUPS, SGU_GROUP_DIM)
    sv = jnp.einsum('gij,bnjgc->bnigc', w_s, vc) + b_s.T[None, None, :, :, None]
    return (u * sv.reshape(B, S, SGU_DIM)) @ w_o


def swiglu_ffn(h, w_gu, w_down):
    g, u = jnp.split(h @ w_gu, 2, axis=-1)
    return (jax.nn.silu(g) * u) @ w_down


def run_trunk(x, c, norm1_g, norm2_g, w_ada, b_ada, fnet_w_o, attn_w_qkv, attn_q_g, attn_k_g, attn_w_o,
              sgu_w_in, sgu_ln_g, sgu_ln_b, sgu_w_s, sgu_b_s, sgu_w_o, ffn_w_gu, ffn_w_down, final_g):
    cs = jax.nn.silu(c)
    for i in range(DEPTH):
        mod = cs @ w_ada[i] + b_ada[i]
        sh1, sc1, g1, sh2, sc2, g2 = jnp.split(mod, 6, axis=-1)
        h = modulate(rms_norm(x, norm1_g[i]), sh1, sc1)
        kind = i % N_MIXERS
        j = i // N_MIXERS
        if kind == 0:
            m = fourier_mixer(h, fnet_w_o[j])
        elif kind == 1:
            m = attention_mixer(h, attn_w_qkv[j], attn_q_g[j], attn_k_g[j], attn_w_o[j])
        else:
            m = spatial_gating_mixer(h, sgu_w_in[j], sgu_ln_g[j], sgu_ln_b[j], sgu_w_s[j], sgu_b_s[j], sgu_w_o[j])
        x = x + g1[:, None, :] * m
        h = modulate(rms_norm(x, norm2_g[i]), sh2, sc2)
        x = x + g2[:, None, :] * swiglu_ffn(h, ffn_w_gu[i], ffn_w_down[i])
    return rms_norm(x, final_g)


def setup_inputs(seed: int = 0) -> dict:
    key = jax.random.key(seed)
    ks = jax.random.split(key, 32)
    n_a = (DEPTH + 2) // 3
    n_b = (DEPTH + 1) // 3
    n_c = DEPTH // 3
    f32 = jnp.float32

    def nrm(k, shape, scale):
        return jax.random.normal(k, shape, f32) * scale

    D = D_MODEL
    return {
        'x_prompt': nrm(ks[0], (BATCH, SEQ, D), 1.0),
        'x_sample': nrm(ks[1], (DEC_BATCH, DEC_SEQ, D), 1.0),
        'c_prompt': nrm(ks[2], (BATCH, D), 1.0),
        'c_sample': nrm(ks[3], (DEC_BATCH, D), 1.0),
        'norm1_g': 1.0 + nrm(ks[4], (DEPTH, D), 0.02),
        'norm2_g': 1.0 + nrm(ks[5], (DEPTH, D), 0.02),
        'w_ada': nrm(ks[6], (DEPTH, D, 6 * D), 0.5 * D ** -0.5),
        'b_ada': nrm(ks[7], (DEPTH, 6 * D), 0.01),
        'fnet_w_o': nrm(ks[8], (n_a, D, D), D ** -0.5),
        'attn_w_qkv': nrm(ks[9], (n_b, D, QKV_DIM), D ** -0.5),
        'attn_q_g': 1.0 + nrm(ks[10], (n_b, HEAD_DIM), 0.02),
        'attn_k_g': 1.0 + nrm(ks[11], (n_b, HEAD_DIM), 0.02),
        'attn_w_o': nrm(ks[12], (n_b, N_HEADS * HEAD_DIM, D), (N_HEADS * HEAD_DIM) ** -0.5),
        'sgu_w_in': nrm(ks[13], (n_c, D, 2 * SGU_DIM), D ** -0.5),
        'sgu_ln_g': 1.0 + nrm(ks[14], (n_c, SGU_DIM), 0.02),
        'sgu_ln_b': nrm(ks[15], (n_c, SGU_DIM), 0.02),
        'sgu_w_s': nrm(ks[16], (n_c, SGU_GROUPS, SGU_CHUNK, SGU_CHUNK), 0.5 * SGU_CHUNK ** -0.5),
        'sgu_b_s': 1.0 + nrm(ks[17], (n_c, SGU_GROUPS, SGU_CHUNK), 0.02),
        'sgu_w_o': nrm(ks[18], (n_c, SGU_DIM, D), SGU_DIM ** -0.5),
        'ffn_w_gu': nrm(ks[19], (DEPTH, D, 2 * FFN_HIDDEN), D ** -0.5),
        'ffn_w_down': nrm(ks[20], (DEPTH, FFN_HIDDEN, D), FFN_HIDDEN ** -0.5),
        'final_g': 1.0 + nrm(ks[21], (D,), 0.02),
    }


def reference(x_prompt, x_sample, c_prompt, c_sample, norm1_g, norm2_g, w_ada, b_ada, fnet_w_o,
              attn_w_qkv, attn_q_g, attn_k_g, attn_w_o, sgu_w_in, sgu_ln_g, sgu_ln_b, sgu_w_s, sgu_b_s,
              sgu_w_o, ffn_w_gu, ffn_w_down, final_g):
    y_prompt = run_trunk(x_prompt, c_prompt, norm1_g, norm2_g, w_ada, b_ada, fnet_w_o, attn_w_qkv, attn_q_g,
                         attn_k_g, attn_w_o, sgu_w_in, sgu_ln_g, sgu_ln_b, sgu_w_s, sgu_b_s, sgu_w_o,
                         ffn_w_gu, ffn_w_down, final_g)
    y_sample = run_trunk(x_sample, c_sample, norm1_g, norm2_g, w_ada, b_ada, fnet_w_o, attn_w_qkv, attn_q_g,
                         attn_k_g, attn_w_o, sgu_w_in, sgu_ln_g, sgu_ln_b, sgu_w_s, sgu_b_s, sgu_w_o,
                         ffn_w_gu, ffn_w_down, final_g)
    return (y_prompt, y_sample)
```

```python
import contextlib
import math
import numpy as np
import ml_dtypes
import concourse.bass as bass
import concourse.mybir as mybir
from concourse.bass_utils import run_bass_kernel_spmd

F32 = mybir.dt.float32
BF16 = mybir.dt.bfloat16
AF = mybir.ActivationFunctionType
ALU = mybir.AluOpType

D = 1024
KC = 8
FH = 2816
HC = 22
TOK = 512
NJ = 4
EPS = 1e-6
SEG = 12000
ATTACH_WAIT = True
COMPUTE = ("pe", "act", "dve", "pool")


class Res:
    __slots__ = ("name", "ws", "rs")

    def __init__(self, name=""):
        self.name = name
        self.ws = []
        self.rs = []


class Op:
    __slots__ = ("eng", "fn", "deps", "sig", "cnt", "key", "is_dma", "sem", "noatt")

    def __init__(self, eng, fn, is_dma, key):
        self.eng = eng
        self.fn = fn
        self.deps = []
        self.sig = is_dma
        self.cnt = 0
        self.key = key
        self.is_dma = is_dma
        self.sem = None
        self.noatt = False


def _covers(a, b):
    if a.is_dma != b.is_dma:
        return False
    if a.is_dma:
        return a.key == b.key
    return a.eng == b.eng


class Prog:
    def __init__(self):
        self.streams = {e: [] for e in ("pe", "act", "dve", "pool", "sp")}
        self.keyq = {}
        self.lastdma = {}
        self.lastcomp = {}
        self.pending = {}

    def _add(self, eng, fn, reads, writes, is_dma=False, key=None):
        op = Op(eng, fn, is_dma, key)
        deps = op.deps
        pend = self.pending.pop(eng, None)
        if pend:
            for p in pend:
                if (not p.is_dma) and (not is_dma) and p.eng == eng:
                    continue
                deps.append(p)
        if is_dma:
            assert self.keyq.setdefault(key, eng) == eng, key
            self.lastdma[key] = op
        else:
            self.lastcomp[eng] = op

        def dep(p, raw):
            if not p.is_dma and not is_dma and p.eng == eng:
                if eng == "pe" or not raw:
                    return
            deps.append(p)

        for r in reads:
            for w in r.ws:
                dep(w, True)
        for r in writes:
            if r.rs:
                for q in r.rs:
                    dep(q, False)
            else:
                for w in r.ws:
                    if w.is_dma and is_dma:
                        continue
                    dep(w, False)
        for r in writes:
            if r.rs:
                r.ws = [op]
                r.rs = []
            else:
                r.ws = [w for w in r.ws if not _covers(w, op)] + [op]
        for r in reads:
            if r in writes:
                continue
            r.rs = [q for q in r.rs if not _covers(q, op)] + [op]
        self.streams[eng].append(op)
        return op

    def pe(self, fn, reads=(), writes=()):
        return self._add("pe", fn, reads, writes)

    def act(self, fn, reads=(), writes=(), noatt=False):
        op = self._add("act", fn, reads, writes)
        op.noatt = noatt
        return op

    def dve(self, fn, reads=(), writes=()):
        return self._add("dve", fn, reads, writes)

    def pool(self, fn, reads=(), writes=()):
        return self._add("pool", fn, reads, writes)

    def dma(self, q, key, fn, reads=(), writes=()):
        return self._add(q, fn, reads, writes, is_dma=True, key=key)

    def barrier(self):
        marks = [op for op in self.lastcomp.values()] + list(self.lastdma.values())
        for e in self.streams:
            self.pending[e] = list(marks)

    def emit(self, nc):
        for s in self.streams.values():
            for op in s:
                for d in op.deps:
                    d.sig = True
        keycnt = {}
        engcnt = {e: 0 for e in COMPUTE}
        sem_names = []
        for e, s in self.streams.items():
            for op in s:
                if not op.sig:
                    continue
                if op.is_dma:
                    keycnt[op.key] = keycnt.get(op.key, 0) + 16
                    op.cnt = keycnt[op.key]
                    op.sem = "k_" + op.key
                else:
                    seg, c = divmod(engcnt[e], SEG)
                    engcnt[e] += 1
                    op.cnt = c + 1
                    op.sem = "%s_%d" % (e, seg)
                if op.sem not in sem_names:
                    sem_names.append(op.sem)
        self.nsem = len(sem_names)
        with contextlib.ExitStack() as st:
            sems = {n: st.enter_context(nc.semaphore(n)) for n in sem_names}
            block = st.enter_context(nc.Block())
            engmap = {"pe": block.tensor, "act": block.scalar, "dve": block.vector,
                      "pool": block.gpsimd, "sp": block.sync}
            for e, s in self.streams.items():
                def body(eng, s=s, e=e):
                    waited = {}
                    for op in s:
                        need = {}
                        for d in op.deps:
                            if d.cnt > need.get(d.sem, 0):
                                need[d.sem] = d.cnt
                        todo = [(sn, v) for sn, v in need.items() if waited.get(sn, 0) < v]
                        for sn, v in todo:
                            waited[sn] = v
                        att = None
                        if ATTACH_WAIT and todo and not op.is_dma and not op.noatt:
                            att = todo.pop()
                        for sn, v in todo:
                            eng.wait_ge(sems[sn], v)
                        ins = op.fn(eng)
                        if att is not None:
                            ins._wait_ge(sems[att[0]], att[1])
                        if op.sig:
                            ins.then_inc(sems[op.sem], 16 if op.is_dma else 1)
                    if e == "sp":
                        for k, v in keycnt.items():
                            eng.wait_ge(sems["k_" + k], v)
                engmap[e](body)


class Slot:
    __slots__ = ("t", "res", "key")

    def __init__(self, t, res, key):
        self.t = t
        self.res = res
        self.key = key


class Ring:
    def __init__(self, nc, name, n, shape, dtype):
        self.slots = [Slot(nc.alloc_sbuf_tensor("%s%d" % (name, i), list(shape), dtype), Res(name), "%s%d" % (name, i))
                      for i in range(n)]
        self.i = 0

    def next(self):
        s = self.slots[self.i % len(self.slots)]
        self.i += 1
        return s


def _consts(seq_lens):
    bf = ml_dtypes.bfloat16
    c = {}
    p = np.arange(128)
    n = np.arange(256)
    cd = np.zeros((128, 2, 512), np.float64)
    for kk in range(2):
        ch = kk * 128 + p
        ang = 2 * np.pi * ((ch[:, None] * n[None, :]) % 256) / 256.0
        cd[:, kk, :256] = np.cos(ang)
        cd[:, kk, 256:] = np.sin(ang)
    c["cd"] = cd.astype(bf)
    pm = np.where(p % 2 == 0, 1.0, -1.0)
    c["pm1"] = np.stack([pm, pm], 1).astype(bf)
    jp = np.zeros((128, 2, 2, 128), np.float32)
    for kold in range(2):
        for pp in range(128):
            cn = (256 - (kold * 128 + pp)) % 256
            jp[pp, kold, cn // 128, cn % 128] = 1.0
    c["jperm"] = jp.astype(bf)
    for S in sorted(set(seq_lens)):
        s = np.arange(S, dtype=np.int64)
        k = (s[:, None] * s[None, :]) % S
        ang = 2 * np.pi * k / float(S)
        cosm = np.cos(ang).astype(bf)
        msin = (-np.sin(ang)).astype(bf)
        both = np.stack([cosm, msin], 0).reshape(2, S // 128, 128, S // 512, 512)
        c["ds%d" % S] = np.ascontiguousarray(both.transpose(3, 2, 1, 0, 4))
    t = np.arange(4096)
    row = (t // 64).astype(np.float32)
    col = (t % 64).astype(np.float32)
    freqs = (1.0 / (np.float32(10000.0) ** (np.arange(0, 64, 2, dtype=np.float32) / np.float32(64)))).astype(np.float32)
    cosT = np.zeros((128, 4096), np.float32)
    sinT = np.zeros((128, 4096), np.float32)
    rm = np.zeros((128, 128), np.float32)
    for d in range(128):
        pos = row if d < 64 else col
        ang = (pos * freqs[d % 32]).astype(np.float32)
        cosT[d] = np.cos(ang)
        sinT[d] = np.sin(ang)
        if d % 64 < 32:
            rm[d, d + 32] = -1.0
        else:
            rm[d, d - 32] = 1.0
    c["ropec"] = cosT
    c["ropes"] = sinT
    c["rmT"] = np.ascontiguousarray(rm.T)
    return c


WSPECS = [("norm1_g", [4, D]), ("norm2_g", [4, D]), ("w_ada", [4, D, 6 * D]), ("b_ada", [4, 6 * D]),
          ("fnet_w_o", [2, D, D]), ("attn_w_qkv", [1, D, 1536]), ("attn_q_g", [1, 128]), ("attn_k_g", [1, 128]),
          ("attn_w_o", [1, D, D]), ("sgu_w_in", [1, D, 2 * D]), ("sgu_ln_g", [1, D]), ("sgu_ln_b", [1, D]),
          ("sgu_w_s", [1, 8, 128, 128]), ("sgu_b_s", [1, 8, 128]), ("sgu_w_o", [1, D, D]),
          ("ffn_w_gu", [4, D, 2 * FH]), ("ffn_w_down", [4, FH, D]), ("final_g", [D])]


def build(SEQS, layers=(0, 1, 2, 3)):
    nc = bass.Bass("TRN2", target_bir_lowering=False)
    NB = len(SEQS)
    NT = sum(SEQS)
    NSUB = NT // 128
    offs = [sum(SEQS[:i]) for i in range(NB)]
    SMAX = max(SEQS)

    def din(name, shape, dt=F32):
        return nc.dram_tensor(name, list(shape), dt, kind="ExternalInput").ap()

    xin = din("xin", [NT, D])
    cvec = din("cvec", [NB, D])
    W = {n: din(n, s) for n, s in WSPECS}
    cd_d = din("cd", [128, 2, 512], BF16)
    pm1_d = din("pm1", [128, 2], BF16)
    jperm_d = din("jperm", [128, 2, 2, 128], BF16)
    ds_d = {S: din("ds%d" % S, [S // 512, 128, S // 128, 2, 512], BF16) for S in sorted(set(SEQS))}
    ropec_d = din("ropec", [128, 4096])
    ropes_d = din("ropes", [128, 4096])
    rmT_d = din("rmT", [128, 128])
    yout = nc.dram_tensor("yout", [NT, D], F32, kind="ExternalOutput").ap()
    X = nc.dram_tensor("Xs", [NT, D], F32, kind="Internal").ap()
    PQ = nc.dram_tensor("PQs", [NT, 2048], BF16, kind="Internal").ap()
    GT = nc.dram_tensor("GTs", [4, 2, NB, 128, D], F32, kind="Internal").ap()

    P = Prog()
    uid = [0]
    XR = [Res("X%d" % i) for i in range(NSUB)]
    PQR = [Res("PQ%d" % i) for i in range(NSUB)]
    GTR = Res("GT")

    def sb(name, shape, dt):
        return nc.alloc_sbuf_tensor(name, list(shape), dt)

    identf = sb("identf", [128, 128], F32)
    ident = sb("ident", [128, 128], BF16)
    onesb = sb("onesb", [128, 128], BF16)
    AA = sb("AA", [128, 4, 2, KC, NB], F32)
    BB = sb("BB", [128, 4, 2, KC, NB], F32)
    fgT = sb("fgT", [128, KC], F32)
    r_const = Res("const")
    r_AB = Res("AB")
    pb = [nc.alloc_psum_tensor("pb%d" % i, [128, 512], F32) for i in range(8)]
    bres = [Res("pb%d" % i) for i in range(8)]

    def pbT(i):
        return pb[i][:].bitcast(BF16).rearrange("p (a n) -> p a n", n=128)

    P.pool(lambda e: e.memset(identf[:], 0.0), writes=[r_const])
    P.pool(lambda e: e.affine_select(out=identf[:], in_=identf[:], pattern=[[-1, 128]], compare_op=ALU.not_equal,
                                     fill=1.0, base=0, channel_multiplier=1), reads=[r_const], writes=[r_const])
    P.pool(lambda e: e.memset(onesb[:], 1.0 / 128.0), writes=[r_const])
    P.dve(lambda e: e.tensor_copy(out=ident[:], in_=identf[:]), reads=[r_const], writes=[r_const])

    ssr = Ring(nc, "ssr", 4, [128, 1], F32)
    sqr = Ring(nc, "sqr", 4, [128, 1], F32)
    rsr = Ring(nc, "rsr", 4, [128, 1], F32)

    def tr_rows(src_ap, R, dst_ap, bank, func=AF.Copy, stage=None):
        P.dma("sp", "trs", lambda e: e.dma_start(out=stage.t[0:R, :], in_=src_ap), writes=[stage.res])
        P.pe(lambda e: e.transpose(out=pb[bank][:, 0:R], in_=stage.t[0:R, :], identity=identf[0:R, 0:R]),
             reads=[stage.res, r_const], writes=[bres[bank]])
        P.act(lambda e: e.activation(out=dst_ap, in_=pb[bank][:, 0:R], func=func), reads=[bres[bank]], writes=[r_AB])

    with contextlib.ExitStack() as ph:
        def sbp(name, shape, dt):
            return ph.enter_context(nc.sbuf_tensor(name, list(shape), dt))
        stage = Slot(sbp("stage", [128, 128], F32), Res("stage"), "trs")
        csT = sbp("csT", [128, NB, KC], F32)
        csT2 = sbp("csT2", [128, KC, NB], F32)
        csrep = sbp("csrep", [128, NB, KC, 128], F32)
        n1T = sbp("n1T", [128, 4, KC], F32)
        n2T = sbp("n2T", [128, 4, KC], F32)
        baT = sbp("baT", [128, 192], F32)
        modT = sbp("modT", [128, 4, 48, NB], F32)
        wring = [Slot(sbp("wblk%d" % i, [128, KC, 512], F32), Res("wblk"), "wblk%d" % i) for i in range(3)]
        wbring = [Slot(sbp("wbb%d" % i, [128, KC, 512], BF16), Res("wbb"), "wbb%d" % i) for i in range(3)]
        csrepb = sbp("csrepb", [128, NB, KC, 128], BF16)
        ngb = 0
        brow = [Slot(sbp("brow%d" % i, [128, 512], F32), Res("brow"), "brow%d" % i) for i in range(2)]
        grow = [Slot(sbp("grow%d" % i, [128, 512], F32), Res("grow"), "grow%d" % i) for i in range(2)]
        r_cs = Res("cs")
        r_mod = Res("mod")

        tr_rows(cvec.rearrange("b (k p) -> (b k) p", p=128), NB * KC, csT[:].rearrange("p b k -> p (b k)"), 0, AF.Silu, stage)
        tr_rows(W["norm1_g"].rearrange("l (k p) -> (l k) p", p=128), 32, n1T[:].rearrange("p l k -> p (l k)"), 1, AF.Copy, stage)
        tr_rows(W["norm2_g"].rearrange("l (k p) -> (l k) p", p=128), 32, n2T[:].rearrange("p l k -> p (l k)"), 0, AF.Copy, stage)
        tr_rows(W["final_g"].rearrange("(k p) -> k p", p=128), 8, fgT[:], 1, AF.Copy, stage)
        bav = W["b_ada"].rearrange("l (c p) -> (l c) p", p=128)
        tr_rows(bav[0:96, :], 96, baT[:, 0:96], 0, AF.Copy, stage)
        tr_rows(bav[96:192, :], 96, baT[:, 96:192], 1, AF.Copy, stage)
        P.dve(lambda e: e.tensor_copy(out=csT2[:], in_=csT[:].rearrange("p b k -> p k b")), reads=[r_AB], writes=[r_cs])
        P.dve(lambda e: e.tensor_copy(out=csrep[:].rearrange("p b k n -> p (b k) n"),
                                      in_=csT[:].rearrange("p b (k o) -> p (b k) o", o=1).broadcast_to([128, NB * KC, 128])),
              reads=[r_AB], writes=[r_cs])
        P.dve(lambda e: e.tensor_copy(out=csrepb[:], in_=csrep[:]), reads=[r_cs], writes=[r_cs])
        nblk = 0
        ngate = 0
        for l in layers:
            for cb in range(12):
                region = cb // 2
                if region in (2, 5):
                    ws = wbring[ngb % 3]
                    ngb += 1
                    P.dma("pool", ws.key, lambda e, ws=ws, l=l, cb=cb: e.dma_start(
                        out=ws.t[:], in_=W["w_ada"][l, :, cb * 512:(cb + 1) * 512].rearrange("(k p) n -> p k n", p=128)),
                        writes=[ws.res])
                else:
                    ws = wring[nblk % 3]
                    nblk += 1
                    P.dma("sp", ws.key, lambda e, ws=ws, l=l, cb=cb: e.dma_start(
                        out=ws.t[:], in_=W["w_ada"][l, :, cb * 512:(cb + 1) * 512].rearrange("(k p) n -> p k n", p=128)),
                        writes=[ws.res])
                if region in (2, 5):
                    which = 0 if region == 2 else 1
                    half = cb % 2
                    br = brow[ngate % 2]
                    P.dma("sp", br.key, lambda e, br=br, l=l, cb=cb: e.dma_start(
                        out=br.t[:], in_=W["b_ada"][l:l + 1, cb * 512:(cb + 1) * 512].partition_broadcast(128)), writes=[br.res])
                    for b in range(NB):
                        bank = 2 + (b % 2)
                        for k in range(KC):
                            P.pe(lambda e, b=b, k=k, bank=bank, ws=ws: e.matmul(
                                pb[bank][:], lhsT=csrepb[:, b, k, :], rhs=ws.t[:, k, :], start=(k == 0), stop=(k == KC - 1)),
                                reads=[r_cs, ws.res], writes=[bres[bank]])
                        gr = grow[ngate % 2]
                        ngate += 1
                        P.dve(lambda e, gr=gr, bank=bank, br=br: e.tensor_tensor(out=gr.t[:], in0=pb[bank][:], in1=br.t[:], op=ALU.add),
                              reads=[bres[bank], br.res], writes=[gr.res])
                        P.dma("sp", gr.key, lambda e, gr=gr, l=l, which=which, b=b, half=half: e.dma_start(
                            out=GT[l, which, b, :, half * 512:(half + 1) * 512], in_=gr.t[:]), reads=[gr.res], writes=[GTR])
                else:
                    bank = 4 + (nblk % 2)
                    for oc in range(4):
                        for k in range(KC):
                            P.pe(lambda e, oc=oc, k=k, bank=bank, ws=ws: e.matmul(
                                pb[bank][:, oc * NB:(oc + 1) * NB], lhsT=ws.t[:, k, oc * 128:(oc + 1) * 128], rhs=csT2[:, k, :],
                                start=(k == 0), stop=(k == KC - 1)), reads=[r_cs, ws.res], writes=[bres[bank]])
                    ch0 = cb * 4
                    P.dve(lambda e, bank=bank, l=l, ch0=ch0: e.tensor_tensor(
                        out=modT[:, l, ch0:ch0 + 4, :], in0=pb[bank][:, 0:4 * NB].rearrange("p (c b) -> p c b", b=NB),
                        in1=baT[:, l * 48 + ch0:l * 48 + ch0 + 4].rearrange("p (c o) -> p c o", o=1).broadcast_to([128, 4, NB]),
                        op=ALU.add), reads=[bres[bank], r_AB], writes=[r_mod])
            for w_, (nT, sc0, sh0) in enumerate(((n1T, 8, 0), (n2T, 32, 24))):
                P.dve(lambda e, l=l, w_=w_, nT=nT, sc0=sc0: e.scalar_tensor_tensor(
                    out=AA[:, l, w_, :, :], in0=modT[:, l, sc0:sc0 + 8, :], scalar=1.0,
                    in1=nT[:, l, :].rearrange("p (k o) -> p k o", o=1).broadcast_to([128, KC, NB]),
                    op0=ALU.add, op1=ALU.mult), reads=[r_mod, r_AB], writes=[r_AB])
                P.dve(lambda e, l=l, w_=w_, sh0=sh0: e.tensor_copy(out=BB[:, l, w_, :, :], in_=modT[:, l, sh0:sh0 + 8, :]),
                      reads=[r_mod], writes=[r_AB])
        P.barrier()

    def seq_of_tile(ti):
        t0 = ti * TOK
        for b in range(NB):
            if offs[b] <= t0 < offs[b] + SEQS[b]:
                return b
        raise AssertionError

    class Tools:
        pass

    def make_tools(ph, nxa=3, nxr=3, nhT=1, nxn=2, batched=False):
        def sbp(name, shape, dt):
            uid[0] += 1
            return ph.enter_context(nc.sbuf_tensor("%s_%d" % (name, uid[0]), list(shape), dt))
        T = Tools()
        T.sbp = sbp
        T.xa = [Slot(sbp("xa%d" % i, [128, D], F32), Res("xa"), "xa%d" % i) for i in range(nxa)]
        T.xr = [Slot(sbp("xr%d" % i, [128, D], F32), Res("xr"), "xr%d" % i) for i in range(nxr)]
        if nxa:
            T.xn = [Slot(sbp("xn%d" % i, [128, D], BF16), Res("xn"), None) for i in range(nxn)]
            T.hTs = [Slot(sbp("hT%d" % i, [128, KC, TOK], BF16), Res("hT"), None) for i in range(nhT)]
        if nxr:
            T.tmp = [Slot(sbp("tmp%d" % i, [128, 512], F32), Res("tmp"), None) for i in range(2)]
            T.G = [Slot(sbp("G%d" % i, [128, D], F32), Res("G"), "G%d" % i) for i in range(1)]
        T.cnt = {"xn": 0, "tmp": 0, "G": 0, "tb": 0, "hT": 0, "ss4": 0}
        T.batched = batched
        if batched:
            T.ss4 = [Slot(sbp("ss4%d" % i, [128, 12], F32), Res("ss4"), None) for i in range(3)]
        return T

    def _nl_emit(T):
        src_, isx = T.nsrc
        pos = T.nl_e
        g = T.norder[pos]
        s = T.xa[pos % len(T.xa)]
        P.dma("sp", s.key, lambda e, s=s, g=g: e.dma_start(out=s.t[:], in_=src_[g * 128:(g + 1) * 128, :]),
              reads=[XR[g]] if isx else [], writes=[s.res])
        T.nl_e += 1

    def norm_setup(T, order, src_, isx, l=None, w_=None, tbanks=(0, 1)):
        T.ntiles_order = list(order)
        T.ntile_i = 0
        T.nparams = (l, w_, tbanks)
        T.norder = [ti * NJ + j for ti in T.ntiles_order for j in range(NJ)]
        T.nsrc = (src_, isx)
        T.nl_e = 0
        T.npos = 0
        for _ in range(min(len(T.xa), len(T.norder))):
            _nl_emit(T)

    def norm_next(T, split=False):
        if T.ntile_i >= len(T.ntiles_order):
            return (None, [], []) if split else None
        ti = T.ntiles_order[T.ntile_i]
        T.ntile_i += 1
        l, w_, tbanks = T.nparams
        hs = T.hTs[T.cnt["hT"] % len(T.hTs)]
        T.cnt["hT"] += 1
        if getattr(T, "batched", False):
            A, B = norm_parts_batched(T, ti, l, w_, tbanks, hs)
        else:
            A, B = norm_parts(T, ti, l, w_, tbanks, hs)
        if split:
            return hs, A, B
        for f_ in A:
            f_()
        for f_ in B:
            f_()
        return hs

    def norm_parts(T, ti, l, w_, tbanks, hs):
        b = seq_of_tile(ti)
        st = {}

        def A(j):
            assert T.norder[T.npos] == ti * NJ + j
            s = T.xa[T.npos % len(T.xa)]
            T.npos += 1
            xn = T.xn[T.cnt["xn"] % len(T.xn)]
            T.cnt["xn"] += 1
            st[j] = xn
            ss = ssr.next()
            sq = sqr.next()
            rs = rsr.next()
            P.act(lambda e, s=s, xn=xn, ss=ss: e.activation(out=xn.t[:], in_=s.t[:], func=AF.Square, accum_out=ss.t[:]),
                  reads=[s.res], writes=[xn.res, ss.res], noatt=True)
            P.act(lambda e, ss=ss, sq=sq: e.activation(out=sq.t[:], in_=ss.t[:], func=AF.Sqrt, bias=EPS, scale=1.0 / D),
                  reads=[ss.res], writes=[sq.res])
            P.dve(lambda e, sq=sq, rs=rs: e.reciprocal(out=rs.t[:], in_=sq.t[:]), reads=[sq.res], writes=[rs.res])
            P.act(lambda e, s=s, xn=xn, rs=rs: e.activation(out=xn.t[:], in_=s.t[:], func=AF.Copy, scale=rs.t[:, 0:1]),
                  reads=[s.res, rs.res], writes=[xn.res])
            if T.nl_e < len(T.norder):
                _nl_emit(T)

        def B(j):
            xn = st[j]
            bank = tbanks[T.cnt["tb"] % len(tbanks)]
            T.cnt["tb"] += 1
            for k in range(KC):
                P.pe(lambda e, xn=xn, k=k, bank=bank: e.transpose(out=pbT(bank)[:, k, :], in_=xn.t[:, k * 128:(k + 1) * 128],
                                                                  identity=ident[:]),
                     reads=[xn.res, r_const], writes=[bres[bank]])
            for k in range(KC):
                P.dve(lambda e, k=k, j=j, bank=bank, l=l, w_=w_, b=b, hs=hs: e.tensor_scalar(
                    out=hs.t[:, k, j * 128:(j + 1) * 128], in0=pbT(bank)[:, k, :],
                    scalar1=AA[:, l, w_, k, b:b + 1], scalar2=BB[:, l, w_, k, b:b + 1], op0=ALU.mult, op1=ALU.add),
                    reads=[bres[bank], r_AB], writes=[hs.res])

        return [lambda j=j: A(j) for j in range(NJ)], [lambda j=j: B(j) for j in range(NJ)]

    def norm_parts_batched(T, ti, l, w_, tbanks, hs):
        b = seq_of_tile(ti)
        st = {}
        ss4 = T.ss4[T.cnt["ss4"] % len(T.ss4)]
        T.cnt["ss4"] += 1

        sl = []

        def A1():
            for j in range(NJ):
                assert T.norder[T.npos] == ti * NJ + j
                s = T.xa[T.npos % len(T.xa)]
                T.npos += 1
                xn = T.xn[T.cnt["xn"] % len(T.xn)]
                T.cnt["xn"] += 1
                st[j] = xn
                sl.append(s)
                P.act(lambda e, s=s, xn=xn, j=j: e.activation(out=xn.t[:], in_=s.t[:], func=AF.Square, accum_out=ss4.t[:, j:j + 1]),
                      reads=[s.res], writes=[xn.res, ss4.res], noatt=True)
            P.act(lambda e: e.activation(out=ss4.t[:, 4:8], in_=ss4.t[:, 0:4], func=AF.Sqrt, bias=EPS, scale=1.0 / D),
                  reads=[ss4.res], writes=[ss4.res])

        def A2():
            P.dve(lambda e: e.reciprocal(out=ss4.t[:, 8:12], in_=ss4.t[:, 4:8]), reads=[ss4.res], writes=[ss4.res])
            for j in range(NJ):
                s, xn = sl[j], st[j]
                P.act(lambda e, s=s, xn=xn, j=j: e.activation(out=xn.t[:], in_=s.t[:], func=AF.Copy, scale=ss4.t[:, 8 + j:9 + j]),
                      reads=[s.res, ss4.res], writes=[xn.res])
                if T.nl_e < len(T.norder):
                    _nl_emit(T)

        def B(j):
            xn = st[j]
            bank = tbanks[T.cnt["tb"] % len(tbanks)]
            T.cnt["tb"] += 1
            for k in range(KC):
                P.pe(lambda e, xn=xn, k=k, bank=bank: e.transpose(out=pbT(bank)[:, k, :], in_=xn.t[:, k * 128:(k + 1) * 128],
                                                                  identity=ident[:]),
                     reads=[xn.res, r_const], writes=[bres[bank]])
            for k in range(KC):
                P.dve(lambda e, k=k, j=j, bank=bank, l=l, w_=w_, b=b, hs=hs: e.tensor_scalar(
                    out=hs.t[:, k, j * 128:(j + 1) * 128], in0=pbT(bank)[:, k, :],
                    scalar1=AA[:, l, w_, k, b:b + 1], scalar2=BB[:, l, w_, k, b:b + 1], op0=ALU.mult, op1=ALU.add),
                    reads=[bres[bank], r_AB], writes=[hs.res])

        return [A1, A2], [lambda j=j: B(j) for j in range(NJ)]

    def load_G(T, l, which, b):
        s = T.G[T.cnt["G"] % len(T.G)]
        T.cnt["G"] += 1
        P.dma("sp", s.key, lambda e, s=s: e.dma_start(out=s.t[:], in_=GT[l, which, b, :, :]), reads=[GTR], writes=[s.res])
        return s

    def _rl_emit(T):
        src_, isx = T.rsrc
        pos = T.rl_e
        g = T.rorder[pos]
        s = T.xr[pos % len(T.xr)]
        P.dma("sp", s.key, lambda e, s=s, g=g: e.dma_start(out=s.t[:], in_=src_[g * 128:(g + 1) * 128, :]),
              reads=[XR[g]] if isx else [], writes=[s.res])
        T.rl_e += 1

    def resid_setup(T, order, src_, isx):
        T.rorder = [ti * NJ + j for ti in order for j in range(NJ)]
        T.rsrc = (src_, isx)
        T.rl_e = 0
        T.rpos = 0
        for _ in range(min(len(T.xr), len(T.rorder))):
            _rl_emit(T)

    def resid_apply(T, ti, j, nh, yb, G):
        assert T.rorder[T.rpos] == ti * NJ + j
        s = T.xr[T.rpos % len(T.xr)]
        tmp = T.tmp[T.cnt["tmp"] % 2]
        T.cnt["tmp"] += 1
        P.dve(lambda e, tmp=tmp, yb=yb, G=G, nh=nh: e.tensor_tensor(out=tmp.t[:], in0=pb[yb][:], in1=G.t[:, nh * 512:(nh + 1) * 512],
                                                                    op=ALU.mult), reads=[bres[yb], G.res], writes=[tmp.res])
        P.pool(lambda e, s=s, tmp=tmp, nh=nh: e.tensor_tensor(out=s.t[:, nh * 512:(nh + 1) * 512], in0=s.t[:, nh * 512:(nh + 1) * 512],
                                                              in1=tmp.t[:], op=ALU.add), reads=[tmp.res, s.res], writes=[s.res])

    def resid_store(T, ti, j, dst=None):
        s = T.xr[T.rpos % len(T.xr)]
        T.rpos += 1
        g = ti * NJ + j
        P.dma("pool", "st_" + s.key, lambda e, s=s, g=g: e.dma_start(out=X[g * 128:(g + 1) * 128, :], in_=s.t[:]),
              reads=[s.res], writes=[XR[g]])
        if T.rl_e < len(T.rorder):
            _rl_emit(T)

    def load_w_cast(dst, dst_res, src2d, key, nchunk_rows, ncols, colsplit):
        cw = ncols // colsplit
        for k in range(nchunk_rows):
            for c in range(colsplit):
                P.dma("pool", key, lambda e, k=k, c=c: e.dma_start(
                    out=dst[:, k, c * cw:(c + 1) * cw], in_=src2d[k * 128:(k + 1) * 128, c * cw:(c + 1) * cw]), writes=[dst_res])

    ntiles = NT // TOK

    def ffn_phase(l):
        with contextlib.ExitStack() as ph:
            T = make_tools(ph, nxa=3, nxr=2, nhT=1, nxn=4)
            Wgu = T.sbp("Wgu", [128, KC, 2 * FH], BF16)
            Wd = T.sbp("Wd", [128, HC, D], BF16)
            aT = T.sbp("aT", [128, HC, TOK], BF16)
            sg = [Slot(T.sbp("sg%d" % i, [128, TOK], BF16), Res("sg"), None) for i in range(2)]
            r_wd, r_aT = Res("wd"), Res("aT")
            r_wgs = [Res("wgu%d" % i) for i in range(4)]
            cw = 2 * FH // 4
            wsrc = W["ffn_w_gu"][l].rearrange("(k p) n -> p k n", p=128)
            for c_ in (0, 2, 1, 3):
                P.dma("pool", "wgu%d" % c_, lambda e, c_=c_: e.dma_start(out=Wgu[:, :, c_ * cw:(c_ + 1) * cw], in_=wsrc[:, :, c_ * cw:(c_ + 1) * cw]),
                      writes=[r_wgs[c_]])
            load_w_cast(Wd, r_wd, W["ffn_w_down"][l], "wd", HC, D, 1)
            norm_setup(T, range(ntiles), X, True, l, 1, (0, 1))
            resid_setup(T, range(ntiles), X, True)
            cur = norm_next(T)
            G = None
            gb_ = -1
            for ti in range(ntiles):
                b = seq_of_tile(ti)
                if b != gb_:
                    G = load_G(T, l, 1, b)
                    gb_ = b
                nxt, An, Bn = norm_next(T, split=True)
                for hc in range(HC):
                    if An and hc in (3, 8, 13, 18):
                        An[(hc - 3) // 5]()
                    pr = hc % 2
                    gbk, ubk = 2 + 2 * pr, 3 + 2 * pr
                    for k in range(KC):
                        P.pe(lambda e, k=k, hc=hc, gbk=gbk, cur=cur: e.matmul(pb[gbk][:], lhsT=Wgu[:, k, hc * 128:(hc + 1) * 128], rhs=cur.t[:, k, :],
                                                                      start=(k == 0), stop=(k == KC - 1)),
                             reads=[r_wgs[0 if hc < 11 else 1], cur.res], writes=[bres[gbk]])
                    for k in range(KC):
                        P.pe(lambda e, k=k, hc=hc, ubk=ubk, cur=cur: e.matmul(pb[ubk][:], lhsT=Wgu[:, k, FH + hc * 128:FH + (hc + 1) * 128],
                                                                      rhs=cur.t[:, k, :], start=(k == 0), stop=(k == KC - 1)),
                             reads=[r_wgs[2 if hc < 11 else 3], cur.res], writes=[bres[ubk]])
                    s_ = sg[hc % 2]
                    P.act(lambda e, s_=s_, gbk=gbk: e.activation(out=s_.t[:], in_=pb[gbk][:], func=AF.Silu),
                          reads=[bres[gbk]], writes=[s_.res])
                    P.dve(lambda e, s_=s_, ubk=ubk, hc=hc: e.tensor_tensor(out=aT[:, hc, :], in0=pb[ubk][:], in1=s_.t[:], op=ALU.mult),
                          reads=[bres[ubk], s_.res], writes=[r_aT])
                for f_ in Bn:
                    f_()
                cur = nxt
                for j in range(NJ):
                    for nh in range(2):
                        yb = 6 + nh
                        for hc in range(HC):
                            P.pe(lambda e, hc=hc, j=j, nh=nh, yb=yb: e.matmul(
                                pb[yb][:], lhsT=aT[:, hc, j * 128:(j + 1) * 128], rhs=Wd[:, hc, nh * 512:(nh + 1) * 512],
                                start=(hc == 0), stop=(hc == HC - 1)), reads=[r_aT, r_wd], writes=[bres[yb]])
                        resid_apply(T, ti, j, nh, yb, G)
                    resid_store(T, ti, j)
            P.barrier()

    def fnet_phase(l, src, src_is_X):
        jw = l // 3
        with contextlib.ExitStack() as ph:
            T = make_tools(ph, nxa=8, nxr=0, nhT=2, nxn=8, batched=True)
            cdt = T.sbp("cdt", [128, 2, 512], BF16)
            r_cd = Res("cd")
            pqs = [Slot(T.sbp("pqs%d" % i, [128, 2048], BF16), Res("pqs"), "pqs%d" % i) for i in range(4)]
            P.dma("sp", "cd", lambda e: e.dma_start(out=cdt[:], in_=cd_d), writes=[r_cd])
            norm_setup(T, range(ntiles), src, src_is_X, l, 0, (0, 1))
            npq = 0
            cur = norm_next(T)
            for ti in range(ntiles):
                nxt, An, Bn = norm_next(T, split=True)
                if An:
                    An[0]()
                for j in range(NJ):
                    if j == 2 and An:
                        An[1]()
                    ps = pqs[npq % 4]
                    npq += 1
                    for g in range(4):
                        bank = 2 + (npq * 4 + g) % 6
                        for kk in range(2):
                            P.pe(lambda e, g=g, kk=kk, j=j, bank=bank, cur=cur: e.matmul(
                                pb[bank][:], lhsT=cur.t[:, 2 * g + kk, j * 128:(j + 1) * 128], rhs=cdt[:, kk, :],
                                start=(kk == 0), stop=(kk == 1)), reads=[cur.res, r_cd], writes=[bres[bank]])
                        if g == 3:
                            P.act(lambda e, ps=ps, g=g, bank=bank: e.activation(out=ps.t[:, g * 512:(g + 1) * 512], in_=pb[bank][:], func=AF.Copy),
                                  reads=[bres[bank]], writes=[ps.res])
                        else:
                            P.dve(lambda e, ps=ps, g=g, bank=bank: e.tensor_copy(out=ps.t[:, g * 512:(g + 1) * 512], in_=pb[bank][:]),
                                  reads=[bres[bank]], writes=[ps.res])
                    gs = ti * NJ + j
                    P.dma("pool", ps.key, lambda e, ps=ps, gs=gs: e.dma_start(out=PQ[gs * 128:(gs + 1) * 128, :], in_=ps.t[:]),
                          reads=[ps.res], writes=[PQR[gs]])
                for f_ in Bn:
                    f_()
                cur = nxt
            P.barrier()
        with contextlib.ExitStack() as ph:
            T2 = make_tools(ph, nxa=0, nxr=3)
            resid_setup(T2, range(ntiles), src, src_is_X)
            NSC = SMAX // 128
            PQh = T2.sbp("PQh", [128, NSC, 1024], BF16)
            FT = T2.sbp("FT", [128, KC, SMAX], BF16)
            Wfo = T2.sbp("Wfo", [128, KC, D], BF16)
            slab = [Slot(T2.sbp("slab%d" % i, [128, 4, 2, 512], BF16), Res("slab"), "slab%d" % i) for i in range(4)]
            r_ft, r_wfo, r_ftB, r_fc = Res("ft"), Res("wfo"), Res("ftB"), Res("fc")
            r_pqg = [Res("pqh%d" % i) for i in range(NSC // 4)]
            pm1 = T2.sbp("pm1", [128, 2], BF16)
            jp = T2.sbp("jp", [128, 2, 2, 128], BF16)
            P.dma("sp", "fc", lambda e: e.dma_start(out=pm1[:], in_=pm1_d), writes=[r_fc])
            P.dma("sp", "fc", lambda e: e.dma_start(out=jp[:], in_=jperm_d), writes=[r_fc])
            nmb = 0
            load_w_cast(Wfo, r_wfo, W["fnet_w_o"][jw], "wfo", KC, D, 1)
            nslab = 0
            nacc = 0
            for b in range(NB):
                S = SEQS[b]
                nsc = S // 128
                nbt = S // 512
                sub0 = offs[b] // 128
                scale = 1.0 / math.sqrt(S * 256.0)
                herm = (nbt % 2 == 0)
                nbd = nbt // 2 if herm else nbt
                for hf in range(2):
                    for q in range(nsc // 4):
                        P.dma("sp", "pqh%d" % q, lambda e, q=q, hf=hf, sub0=sub0: e.dma_start(
                            out=PQh[:, q * 4:(q + 1) * 4, :],
                            in_=PQ[(sub0 + q * 4) * 128:(sub0 + q * 4 + 4) * 128, hf * 1024:(hf + 1) * 1024].rearrange("(c p) n -> p c n", p=128)),
                            reads=[PQR[sub0 + q * 4 + i] for i in range(4)], writes=[r_pqg[q]])
                    for bt in range(nbd):
                        base = 4 * (nacc % 2)
                        nacc += 1
                        for q in range(nsc // 4):
                            sl = slab[nslab % 4]
                            nslab += 1
                            P.dma("sp", sl.key, lambda e, sl=sl, S=S, bt=bt, q=q: e.dma_start(
                                out=sl.t[:], in_=ds_d[S][bt, :, q * 4:(q + 1) * 4, :, :]), writes=[sl.res])
                            for s4 in range(4):
                                sc = q * 4 + s4
                                for cc in range(4):
                                    gl = cc // 2
                                    c0 = gl * 512 + (cc % 2) * 128
                                    P.pe(lambda e, sl=sl, s4=s4, sc=sc, cc=cc, c0=c0, base=base: e.matmul(
                                        pb[base + cc][:], lhsT=PQh[:, sc, c0:c0 + 128], rhs=sl.t[:, s4, 0, :], start=(sc == 0), stop=False),
                                        reads=[r_pqg[sc // 4], sl.res], writes=[bres[base + cc]])
                                    P.pe(lambda e, sl=sl, s4=s4, sc=sc, cc=cc, c0=c0, base=base, nsc=nsc: e.matmul(
                                        pb[base + cc][:], lhsT=PQh[:, sc, c0 + 256:c0 + 384], rhs=sl.t[:, s4, 1, :], start=False,
                                        stop=(sc == nsc - 1)), reads=[r_pqg[sc // 4], sl.res], writes=[bres[base + cc]])
                        for cc in range(4):
                            fo = FT[:, hf * 4 + cc, bt * 512:(bt + 1) * 512]
                            if cc % 2 == 0:
                                P.act(lambda e, fo=fo, base=base, cc=cc, scale=scale: e.activation(out=fo, in_=pb[base + cc][:], func=AF.Copy, scale=scale),
                                      reads=[bres[base + cc]], writes=[r_ft])
                            else:
                                P.dve(lambda e, fo=fo, base=base, cc=cc, scale=scale: e.tensor_scalar(out=fo, in0=pb[base + cc][:], scalar1=scale, scalar2=None, op0=ALU.mult),
                                      reads=[bres[base + cc]], writes=[r_ft])
                    if herm:
                        base = 4 * (nacc % 2)
                        nacc += 1
                        for cc in range(4):
                            c0 = (cc // 2) * 512 + (cc % 2) * 128
                            for sc in range(nsc):
                                P.pe(lambda e, sc=sc, cc=cc, c0=c0, base=base, nsc=nsc: e.matmul(
                                    pb[base][:, 2 * cc:2 * cc + 2], lhsT=PQh[:, sc, c0:c0 + 128], rhs=pm1[:, 0:2], start=(sc == 0),
                                    stop=(sc == nsc - 1)), reads=[r_pqg[sc // 4], r_fc], writes=[bres[base]])
                            P.dve(lambda e, cc=cc, base=base, hf=hf, S=S, scale=scale: e.tensor_scalar(
                                out=FT[:, hf * 4 + cc, S // 2:S // 2 + 1], in0=pb[base][:, 2 * cc:2 * cc + 1], scalar1=scale, scalar2=None,
                                op0=ALU.mult), reads=[bres[base]], writes=[r_ft])
                if herm:
                    for g in range(4):
                        for m in range(nbd):
                            o0 = S // 2 + 512 * m
                            a0 = S // 2 - 512 * m - 511
                            for knew in range(2):
                                bank = nmb % 8
                                nmb += 1
                                for kold in range(2):
                                    P.pe(lambda e, g=g, kold=kold, knew=knew, a0=a0, bank=bank: e.matmul(
                                        pb[bank][:], lhsT=jp[:, kold, knew, :], rhs=FT[:, 2 * g + kold, a0:a0 + 512], start=(kold == 0),
                                        stop=(kold == 1)), reads=[r_ft, r_fc], writes=[bres[bank]])
                                if m == 0:
                                    fo = FT[:, 2 * g + knew, o0 + 1:o0 + 512]
                                    fi = pb[bank][:, 0:511][:, ::-1]
                                else:
                                    fo = FT[:, 2 * g + knew, o0:o0 + 512]
                                    fi = pb[bank][:, ::-1]
                                if knew == 0:
                                    P.act(lambda e, fo=fo, fi=fi: e.activation(out=fo, in_=fi, func=AF.Copy), reads=[bres[bank]], writes=[r_ftB])
                                else:
                                    P.dve(lambda e, fo=fo, fi=fi: e.tensor_copy(out=fo, in_=fi), reads=[bres[bank]], writes=[r_ftB])
                G = load_G(T2, l, 0, b)
                for tt in range(nbt):
                    ti = offs[b] // TOK + tt
                    for j in range(NJ):
                        tok0 = tt * TOK + j * 128
                        for nh in range(2):
                            yb = nh
                            for k in range(KC):
                                P.pe(lambda e, k=k, tok0=tok0, nh=nh, yb=yb: e.matmul(
                                    pb[yb][:], lhsT=FT[:, k, tok0:tok0 + 128], rhs=Wfo[:, k, nh * 512:(nh + 1) * 512],
                                    start=(k == 0), stop=(k == KC - 1)), reads=[r_ft, r_ftB, r_wfo], writes=[bres[yb]])
                            resid_apply(T2, ti, j, nh, yb, G)
                        resid_store(T2, ti, j)
            P.barrier()

    def attn_phase(l, src, src_is_X):
        with contextlib.ExitStack() as ph:
            T = make_tools(ph, nxa=3, nxr=3, nhT=2, nxn=4)
            sbp = T.sbp
            NSC = SMAX // 128
            norder = []
            for b_ in range(NB):
                tl = list(range(offs[b_] // TOK, (offs[b_] + SEQS[b_]) // TOK))
                norder += tl + tl
            norm_setup(T, norder, src, src_is_X, l, 0, (0, 1))
            resid_setup(T, range(ntiles), src, src_is_X)
            Wqkv = sbp("Wqkv", [128, KC, 1536], BF16)
            Wao = sbp("Wao", [128, KC, D], BF16)
            KT = sbp("KT", [128, 2, SMAX], BF16)
            V = sbp("V", [128, NSC, 256], BF16)
            QT = sbp("QT", [128, 8, TOK], BF16)
            OT = sbp("OT", [128, 8, TOK], BF16)
            rmT = sbp("rmT", [128, 128], F32)
            gq = sbp("gq", [128, 2], F32)
            NR = 3
            cst = [Slot(sbp("cst%d" % i, [128, 2, TOK], F32), Res("cst"), "cst%d" % i) for i in range(2)]
            qraw = [Slot(sbp("qraw%d" % i, [128, TOK], F32), Res("qraw"), None) for i in range(NR)]
            qsq = [Slot(sbp("qsq%d" % i, [128, TOK], BF16), Res("qsq"), None) for i in range(NR)]
            qrs = [Slot(sbp("qrs%d" % i, [128, TOK], F32), Res("qrs"), None) for i in range(NR)]
            qn = [Slot(sbp("qn%d" % i, [128, TOK], F32), Res("qn"), None) for i in range(NR)]
            qt1 = [Slot(sbp("qt1%d" % i, [128, TOK], F32), Res("qt1"), None) for i in range(NR)]
            NPT = 6
            pT = [Slot(sbp("pT%d" % i, [128, TOK], BF16), Res("pT"), None) for i in range(NPT)]
            rden = [Slot(sbp("rden%d" % i, [128, TOK], F32), Res("rden"), None) for i in range(2)]
            r_w, r_kt, r_v, r_qt, r_ot, r_g = Res("wqkv"), Res("kt"), Res("v"), Res("qt"), Res("ot"), Res("gq")
            load_w_cast(Wqkv, r_w, W["attn_w_qkv"][0], "wqkv", KC, 1536, 1)
            load_w_cast(Wao, r_w, W["attn_w_o"][0], "wqkv", KC, D, 1)
            P.dma("sp", "gq", lambda e: e.dma_start(out=rmT[:], in_=rmT_d), writes=[r_g])
            P.dma("sp", "gq", lambda e: e.dma_start(out=gq[:, 0:1], in_=W["attn_q_g"].rearrange("o p -> p o")), writes=[r_g])
            P.dma("sp", "gq", lambda e: e.dma_start(out=gq[:, 1:2], in_=W["attn_k_g"].rearrange("o p -> p o")), writes=[r_g])
            P.dve(lambda e: e.tensor_scalar(out=gq[:, 0:1], in0=gq[:, 0:1], scalar1=128.0 ** -0.5, scalar2=None, op0=ALU.mult),
                  reads=[r_g], writes=[r_g])
            cnt = {"h": 0, "cs": 0, "p": 0, "rd": 0, "u": 0}

            def load_cs(tloc):
                c = cst[cnt["cs"] % 2]
                cnt["cs"] += 1
                P.dma("sp", c.key, lambda e, c=c, tloc=tloc: e.dma_start(out=c.t[:, 0, :], in_=ropec_d[:, tloc:tloc + TOK]), writes=[c.res])
                P.dma("sp", c.key, lambda e, c=c, tloc=tloc: e.dma_start(out=c.t[:, 1, :], in_=ropes_d[:, tloc:tloc + TOK]), writes=[c.res])
                return c

            def head_pipeline(jobs, hs, c):
                n = len(jobs)
                base = cnt["h"]
                cnt["h"] += n

                def s1(i):
                    col0 = jobs[i][0]
                    r = (base + i) % NR
                    bk = 2 + (base + i) % 2
                    for k in range(KC):
                        P.pe(lambda e, k=k, col0=col0, bk=bk: e.matmul(pb[bk][:], lhsT=Wqkv[:, k, col0:col0 + 128], rhs=hs.t[:, k, :],
                                                                       start=(k == 0), stop=(k == KC - 1)), reads=[r_w, hs.res], writes=[bres[bk]])
                    P.act(lambda e, r=r, bk=bk: e.activation(out=qraw[r].t[:], in_=pb[bk][:], func=AF.Copy), reads=[bres[bk]], writes=[qraw[r].res])
                    P.act(lambda e, r=r, bk=bk: e.activation(out=qsq[r].t[:], in_=pb[bk][:], func=AF.Square), reads=[bres[bk]], writes=[qsq[r].res])

                def s2(i):
                    gcol = jobs[i][1]
                    r = (base + i) % NR
                    bk = 4 + (base + i) % 2
                    P.pe(lambda e, r=r, bk=bk: e.matmul(pb[bk][:], lhsT=onesb[:], rhs=qsq[r].t[:], start=True, stop=True),
                         reads=[r_const, qsq[r].res], writes=[bres[bk]])
                    P.act(lambda e, r=r, bk=bk: e.activation(out=qrs[r].t[:], in_=pb[bk][:], func=AF.Sqrt, bias=EPS, scale=1.0),
                          reads=[bres[bk]], writes=[qrs[r].res])
                    P.dve(lambda e, r=r: e.reciprocal(out=qrs[r].t[:], in_=qrs[r].t[:]), reads=[qrs[r].res], writes=[qrs[r].res])
                    P.dve(lambda e, r=r, gcol=gcol: e.scalar_tensor_tensor(out=qn[r].t[:], in0=qraw[r].t[:], scalar=gq[:, gcol:gcol + 1], in1=qrs[r].t[:],
                                                                           op0=ALU.mult, op1=ALU.mult), reads=[qraw[r].res, qrs[r].res, r_g], writes=[qn[r].res])

                def s3(i):
                    out_ap, out_res = jobs[i][2], jobs[i][3]
                    r = (base + i) % NR
                    bk = 6 + (base + i) % 2
                    P.pe(lambda e, r=r, bk=bk: e.matmul(pb[bk][:], lhsT=rmT[:], rhs=qn[r].t[:], start=True, stop=True),
                         reads=[r_g, qn[r].res], writes=[bres[bk]])
                    P.dve(lambda e, r=r, bk=bk: e.tensor_tensor(out=qt1[r].t[:], in0=pb[bk][:], in1=c.t[:, 1, :], op=ALU.mult),
                          reads=[bres[bk], c.res], writes=[qt1[r].res])
                    P.pool(lambda e, r=r: e.tensor_tensor(out=qn[r].t[:], in0=qn[r].t[:], in1=c.t[:, 0, :], op=ALU.mult),
                           reads=[qn[r].res, c.res], writes=[qn[r].res])
                    P.dve(lambda e, r=r, out_ap=out_ap: e.tensor_tensor(out=out_ap, in0=qn[r].t[:], in1=qt1[r].t[:], op=ALU.add),
                          reads=[qn[r].res, qt1[r].res], writes=[out_res])

                for step in range(n + 2):
                    if step < n:
                        s1(step)
                    if 0 <= step - 1 < n:
                        s2(step - 1)
                    if 0 <= step - 2 < n:
                        s3(step - 2)

            def head_stage_closures(jobs, hs, c, bk):
                out = []
                for (col0, gcol, out_ap, out_res) in jobs:
                    box = {}

                    def s1(col0=col0, box=box):
                        r = cnt["h"] % NR
                        cnt["h"] += 1
                        box["r"] = r
                        for k in range(KC):
                            P.pe(lambda e, k=k, col0=col0: e.matmul(pb[bk][:], lhsT=Wqkv[:, k, col0:col0 + 128], rhs=hs.t[:, k, :],
                                                                    start=(k == 0), stop=(k == KC - 1)), reads=[r_w, hs.res], writes=[bres[bk]])
                        P.dve(lambda e, r=r: e.tensor_copy(out=qraw[r].t[:], in_=pb[bk][:]), reads=[bres[bk]], writes=[qraw[r].res])
                        P.dve(lambda e, r=r: e.tensor_tensor(out=qsq[r].t[:], in0=pb[bk][:], in1=qraw[r].t[:], op=ALU.mult),
                              reads=[bres[bk], qraw[r].res], writes=[qsq[r].res])

                    def s2(gcol=gcol, box=box):
                        r = box["r"]
                        P.pe(lambda e, r=r: e.matmul(pb[bk][:], lhsT=onesb[:], rhs=qsq[r].t[:], start=True, stop=True),
                             reads=[r_const, qsq[r].res], writes=[bres[bk]])
                        P.act(lambda e, r=r: e.activation(out=qrs[r].t[:], in_=pb[bk][:], func=AF.Sqrt, bias=EPS, scale=1.0),
                              reads=[bres[bk]], writes=[qrs[r].res])
                        P.dve(lambda e, r=r: e.reciprocal(out=qrs[r].t[:], in_=qrs[r].t[:]), reads=[qrs[r].res], writes=[qrs[r].res])
                        P.dve(lambda e, r=r, gcol=gcol: e.scalar_tensor_tensor(out=qn[r].t[:], in0=qraw[r].t[:], scalar=gq[:, gcol:gcol + 1], in1=qrs[r].t[:],
                                                                               op0=ALU.mult, op1=ALU.mult), reads=[qraw[r].res, qrs[r].res, r_g], writes=[qn[r].res])

                    def s3(out_ap=out_ap, out_res=out_res, box=box):
                        r = box["r"]
                        P.pe(lambda e, r=r: e.matmul(pb[bk][:], lhsT=rmT[:], rhs=qn[r].t[:], start=True, stop=True),
                             reads=[r_g, qn[r].res], writes=[bres[bk]])
                        P.dve(lambda e, r=r: e.tensor_tensor(out=qt1[r].t[:], in0=pb[bk][:], in1=c.t[:, 1, :], op=ALU.mult),
                              reads=[bres[bk], c.res], writes=[qt1[r].res])
                        P.pool(lambda e, r=r: e.tensor_tensor(out=qn[r].t[:], in0=qn[r].t[:], in1=c.t[:, 0, :], op=ALU.mult),
                               reads=[qn[r].res, c.res], writes=[qn[r].res])
                        P.pool(lambda e, r=r, out_ap=out_ap: e.tensor_tensor(out=out_ap, in0=qn[r].t[:], in1=qt1[r].t[:], op=ALU.add),
                               reads=[qn[r].res, qt1[r].res], writes=[out_res])
                    out += [s1, s2, s3]
                return out

            QTs = [Slot(QT, r_qt, None), Slot(sbp("QTb", [128, 8, TOK], BF16), Res("qtb"), None)]
            nq = 0
            cur = norm_next(T)
            for b in range(NB):
                S = SEQS[b]
                nsc = S // 128
                nbt = S // TOK
                t0i = offs[b] // TOK
                for tt in range(nbt):
                    c = load_cs(tt * TOK)
                    nxt, An, Bn = norm_next(T, split=True)
                    for f_ in An:
                        f_()
                    head_pipeline([(1024 + kv * 128, 1, KT[:, kv, tt * TOK:(tt + 1) * TOK], r_kt) for kv in range(2)], cur, c)
                    for j in range(NJ):
                        vb = 4 + (j % 2)
                        for k in range(KC):
                            P.pe(lambda e, k=k, j=j, vb=vb, cur=cur: e.matmul(pb[vb][:, 0:256], lhsT=cur.t[:, k, j * 128:(j + 1) * 128],
                                                                     rhs=Wqkv[:, k, 1280:1536], start=(k == 0), stop=(k == KC - 1)),
                                 reads=[r_w, cur.res], writes=[bres[vb]])
                        P.dve(lambda e, j=j, vb=vb, tt=tt: e.tensor_copy(out=V[:, tt * NJ + j, :], in_=pb[vb][:, 0:256]),
                              reads=[bres[vb]], writes=[r_v])
                    for f_ in Bn:
                        f_()
                    cur = nxt
                G = load_G(T, l, 0, b)
                units = [(h, sc) for h in range(8) for sc in range(nsc)]
                nu = len(units)
                LA = 2
                SB = (0, 1, 2)
                SCR = 3
                OBDB = ((5, 6), (7, 4))
                qcur = QTs[nq % 2]
                nq += 1
                c = load_cs(0)
                head_pipeline([(h * 128, 0, qcur.t[:, h, :], qcur.res) for h in range(8)], cur, c)
                for tt in range(nbt):
                    ti = t0i + tt
                    inject = {}
                    qnext = None
                    inj = nu >= 200
                    if tt + 1 < nbt and inj:
                        T.nparams = (l, 0, (SCR,))
                        hs_n, An, Bn = norm_next(T, split=True)
                        T.nparams = (l, 0, (0, 1))
                        qnext = QTs[nq % 2]
                        nq += 1
                        cn = load_cs((tt + 1) * TOK)
                        stages = head_stage_closures([(h * 128, 0, qnext.t[:, h, :], qnext.res) for h in range(8)], hs_n, cn, SCR)
                        evs = An + Bn + stages
                        step = max(1, (nu - 8) // len(evs))
                        for i_, f_ in enumerate(evs):
                            inject.setdefault(4 + i_ * step, []).append(f_)
                    pslot = {}

                    def qk(u, qcur=qcur):
                        h, sc = units[u]
                        kv = h // 4
                        sb_ = SB[(cnt["u"] + u) % len(SB)]
                        P.pe(lambda e, sc=sc, sb_=sb_, kv=kv, h=h, qcur=qcur: e.matmul(pb[sb_][:], lhsT=KT[:, kv, sc * 128:(sc + 1) * 128], rhs=qcur.t[:, h, :],
                                                                             start=True, stop=True), reads=[r_kt, qcur.res], writes=[bres[sb_]])
                        p_ = pT[cnt["p"] % NPT]
                        cnt["p"] += 1
                        P.act(lambda e, p_=p_, sb_=sb_: e.activation(out=p_.t[:], in_=pb[sb_][:], func=AF.Exp),
                              reads=[bres[sb_]], writes=[p_.res])
                        pslot[u] = p_

                    def pv(u, nsc=nsc):
                        h, sc = units[u]
                        kv = h // 4
                        ob, db = OBDB[h % 2]
                        p_ = pslot.pop(u)
                        P.pe(lambda e, sc=sc, p_=p_, kv=kv, nsc=nsc, ob=ob: e.matmul(pb[ob][:], lhsT=V[:, sc, kv * 128:(kv + 1) * 128], rhs=p_.t[:],
                                                                              start=(sc == 0), stop=(sc == nsc - 1)), reads=[r_v, p_.res], writes=[bres[ob]])
                        P.pe(lambda e, sc=sc, p_=p_, nsc=nsc, db=db: e.matmul(pb[db][:], lhsT=onesb[:], rhs=p_.t[:],
                                                                       start=(sc == 0), stop=(sc == nsc - 1)), reads=[r_const, p_.res], writes=[bres[db]])
                        if sc == nsc - 1:
                            rd = rden[cnt["rd"] % 2]
                            cnt["rd"] += 1
                            P.dve(lambda e, rd=rd, db=db: e.reciprocal(out=rd.t[:], in_=pb[db][:]), reads=[bres[db]], writes=[rd.res])
                            P.dve(lambda e, rd=rd, h=h, ob=ob: e.scalar_tensor_tensor(out=OT[:, h, :], in0=pb[ob][:], scalar=1.0 / 128.0, in1=rd.t[:],
                                                                                      op0=ALU.mult, op1=ALU.mult), reads=[bres[ob], rd.res], writes=[r_ot])

                    for u in range(min(LA, nu)):
                        qk(u)
                    for u in range(nu):
                        if u + LA < nu:
                            qk(u + LA)
                        pv(u)
                        for f_ in inject.pop(u, []):
                            f_()
                    for k_ in sorted(inject):
                        for f_ in inject[k_]:
                            f_()
                    cnt["u"] += nu
                    if tt + 1 < nbt and not inj:
                        cur = norm_next(T)
                        qnext = QTs[nq % 2]
                        nq += 1
                        cn = load_cs((tt + 1) * TOK)
                        head_pipeline([(h * 128, 0, qnext.t[:, h, :], qnext.res) for h in range(8)], cur, cn)
                    if tt + 1 == nbt:
                        cur = norm_next(T)
                    for j in range(NJ):
                        for nh in range(2):
                            yb = (3, 5)[nh]
                            for h in range(8):
                                P.pe(lambda e, h=h, j=j, nh=nh, yb=yb: e.matmul(
                                    pb[yb][:], lhsT=OT[:, h, j * 128:(j + 1) * 128], rhs=Wao[:, h, nh * 512:(nh + 1) * 512],
                                    start=(h == 0), stop=(h == 7)), reads=[r_ot, r_w], writes=[bres[yb]])
                            resid_apply(T, ti, j, nh, yb, G)
                        resid_store(T, ti, j)
                    qcur = qnext
            P.barrier()

    def sgu_phase(l, src, src_is_X):
        with contextlib.ExitStack() as ph:
            T = make_tools(ph, nxa=4, nxr=3, nhT=2, nxn=4)
            sbp = T.sbp
            Win = sbp("Win", [128, KC, 2 * D], BF16)
            Wso = sbp("Wso", [128, KC, D], BF16)
            WsT = sbp("WsT", [128, 8, 128], BF16)
            wstage = sbp("wstage", [128, 8, 128], F32)
            BS = sbp("BS", [128, 8, 128], F32)
            LNG = sbp("LNG", [128, D], F32)
            LNB = sbp("LNB", [128, D], F32)
            uT = sbp("uT", [128, KC, TOK], F32)
            aT = sbp("aT", [128, KC, TOK], BF16)
            NV = 4
            vg = [Slot(sbp("vg%d" % i, [128, D], F32), Res("vg"), None) for i in range(NV)]
            vnb = [Slot(sbp("vnb%d" % i, [128, D], BF16), Res("vnb"), None) for i in range(NV)]
            st6 = [Slot(sbp("st6%d" % i, [128, 2, 6], F32), Res("st6"), None) for i in range(NV)]
            mv = [Slot(sbp("mv%d" % i, [128, 2], F32), Res("mv"), None) for i in range(NV)]
            sd = [Slot(sbp("sd%d" % i, [128, 1], F32), Res("sd"), None) for i in range(NV)]
            svt = [Slot(sbp("svt%d" % i, [128, 4, 128], F32), Res("svt"), None) for i in range(4)]
            r_w, r_c, r_u, r_a = Res("w"), Res("c"), Res("u"), Res("a")
            load_w_cast(Win, r_w, W["sgu_w_in"][0], "wsgu", KC, 2 * D, 1)
            load_w_cast(Wso, r_w, W["sgu_w_o"][0], "wsgu", KC, D, 1)
            P.dma("sp", "sguc", lambda e: e.dma_start(out=wstage[:], in_=W["sgu_w_s"][0].rearrange("g i j -> i g j")), writes=[r_c])
            P.dma("sp", "sguc", lambda e: e.dma_start(out=BS[:].rearrange("p g i -> p (g i)"),
                                                     in_=W["sgu_b_s"].rearrange("o g i -> o (g i)").partition_broadcast(128)), writes=[r_c])
            P.dma("sp", "sguc", lambda e: e.dma_start(out=LNG[:], in_=W["sgu_ln_g"].partition_broadcast(128)), writes=[r_c])
            P.dma("sp", "sguc", lambda e: e.dma_start(out=LNB[:], in_=W["sgu_ln_b"].partition_broadcast(128)), writes=[r_c])
            for g in range(8):
                bank = 2 + (g % 2)
                P.pe(lambda e, g=g, bank=bank: e.transpose(out=pb[bank][:, 0:128], in_=wstage[:, g, :], identity=identf[:]),
                     reads=[r_c, r_const], writes=[bres[bank]])
                P.act(lambda e, g=g, bank=bank: e.activation(out=WsT[:, g, :], in_=pb[bank][:, 0:128], func=AF.Copy),
                      reads=[bres[bank]], writes=[r_c])
            G = None
            gb_ = -1
            cntv = {"nv": 0, "nsv": 0}
            norm_setup(T, range(ntiles), src, src_is_X, l, 0, (0, 1))
            resid_setup(T, range(ntiles), src, src_is_X)

            def emit_v(hs):
                vi = []
                for j in range(NJ):
                    i = cntv["nv"] % NV
                    cntv["nv"] += 1
                    vi.append(i)
                    for nh in range(2):
                        bank = 4 + 2 * (j % 2) + nh
                        for k in range(KC):
                            P.pe(lambda e, k=k, j=j, nh=nh, bank=bank: e.matmul(
                                pb[bank][:], lhsT=hs.t[:, k, j * 128:(j + 1) * 128], rhs=Win[:, k, D + nh * 512:D + (nh + 1) * 512],
                                start=(k == 0), stop=(k == KC - 1)), reads=[r_w, hs.res], writes=[bres[bank]])
                        P.act(lambda e, nh=nh, bank=bank, i=i: e.activation(out=vg[i].t[:, nh * 512:(nh + 1) * 512], in_=pb[bank][:], func=AF.Gelu),
                              reads=[bres[bank]], writes=[vg[i].res])
                    for nh in range(2):
                        P.dve(lambda e, nh=nh, i=i: e.bn_stats(out=st6[i].t[:, nh, :], in_=vg[i].t[:, nh * 512:(nh + 1) * 512]),
                              reads=[vg[i].res], writes=[st6[i].res])
                    P.dve(lambda e, i=i: e.bn_aggr(out=mv[i].t[:], in_=st6[i].t[:].rearrange("p a b -> p (a b)")),
                          reads=[st6[i].res], writes=[mv[i].res])
                    P.act(lambda e, i=i: e.activation(out=sd[i].t[:], in_=mv[i].t[:, 1:2], func=AF.Sqrt, bias=EPS, scale=1.0),
                          reads=[mv[i].res], writes=[sd[i].res])
                    P.dve(lambda e, i=i: e.reciprocal(out=sd[i].t[:], in_=sd[i].t[:]), reads=[sd[i].res], writes=[sd[i].res])
                    P.dve(lambda e, i=i: e.tensor_scalar(out=vg[i].t[:], in0=vg[i].t[:], scalar1=mv[i].t[:, 0:1], scalar2=sd[i].t[:, 0:1],
                                                         op0=ALU.subtract, op1=ALU.mult), reads=[vg[i].res, mv[i].res, sd[i].res], writes=[vg[i].res])
                    P.pool(lambda e, i=i: e.tensor_tensor(out=vg[i].t[:], in0=vg[i].t[:], in1=LNG[:], op=ALU.mult),
                           reads=[vg[i].res, r_c], writes=[vg[i].res])
                    P.pool(lambda e, i=i: e.tensor_tensor(out=vnb[i].t[:], in0=vg[i].t[:], in1=LNB[:], op=ALU.add),
                           reads=[vg[i].res, r_c], writes=[vnb[i].res])
                return vi

            cur = norm_next(T)
            vi_cur = emit_v(cur)
            for ti in range(ntiles):
                b = seq_of_tile(ti)
                if b != gb_:
                    G = load_G(T, l, 0, b)
                    gb_ = b
                nxt, An, Bn = norm_next(T, split=True)
                for uc in range(KC):
                    if An and uc in (1, 3, 5, 7):
                        An[(uc - 1) // 2]()
                    bank = 2 + (uc % 2)
                    for k in range(KC):
                        P.pe(lambda e, k=k, uc=uc, bank=bank, cur=cur: e.matmul(pb[bank][:], lhsT=Win[:, k, uc * 128:(uc + 1) * 128], rhs=cur.t[:, k, :],
                                                                        start=(k == 0), stop=(k == KC - 1)), reads=[r_w, cur.res], writes=[bres[bank]])
                    P.act(lambda e, uc=uc, bank=bank: e.activation(out=uT[:, uc, :], in_=pb[bank][:], func=AF.Gelu),
                          reads=[bres[bank]], writes=[r_u])
                for f_ in Bn:
                    f_()
                for j in range(NJ):
                    i = vi_cur[j]
                    for gh in range(2):
                        bank = 2 + gh
                        for g4 in range(4):
                            g = gh * 4 + g4
                            P.pe(lambda e, g=g, g4=g4, i=i, bank=bank: e.matmul(pb[bank][:, g4 * 128:(g4 + 1) * 128], lhsT=vnb[i].t[:, g * 128:(g + 1) * 128],
                                                                                 rhs=WsT[:, g, :], start=True, stop=True),
                                 reads=[vnb[i].res, r_c], writes=[bres[bank]])
                        sv = svt[cntv["nsv"] % 4]
                        cntv["nsv"] += 1
                        P.dve(lambda e, sv=sv, gh=gh, bank=bank: e.tensor_tensor(out=sv.t[:], in0=pb[bank][:].rearrange("p (g i) -> p g i", i=128),
                                                                                 in1=BS[:, gh * 4:(gh + 1) * 4, :], op=ALU.add),
                              reads=[bres[bank], r_c], writes=[sv.res])
                        P.pool(lambda e, sv=sv, gh=gh, j=j: e.tensor_tensor(out=aT[:, gh * 4:(gh + 1) * 4, j * 128:(j + 1) * 128], in0=sv.t[:],
                                                                            in1=uT[:, gh * 4:(gh + 1) * 4, j * 128:(j + 1) * 128], op=ALU.mult),
                               reads=[sv.res, r_u], writes=[r_a])
                vi_next = emit_v(nxt) if nxt is not None else None
                for j in range(NJ):
                    for nh in range(2):
                        yb = 2 + nh
                        for g in range(8):
                            P.pe(lambda e, g=g, j=j, nh=nh, yb=yb: e.matmul(pb[yb][:], lhsT=aT[:, g, j * 128:(j + 1) * 128],
                                                                             rhs=Wso[:, g, nh * 512:(nh + 1) * 512], start=(g == 0), stop=(g == 7)),
                                 reads=[r_a, r_w], writes=[bres[yb]])
                        resid_apply(T, ti, j, nh, yb, G)
                    resid_store(T, ti, j)
                cur = nxt
                vi_cur = vi_next
            P.barrier()

    def final_phase(src, src_is_X):
        with contextlib.ExitStack() as ph:
            def sbp(name, shape, dt):
                uid[0] += 1
                return ph.enter_context(nc.sbuf_tensor("%s_%d" % (name, uid[0]), list(shape), dt))
            FG = sbp("FG", [128, D], F32)
            r_fg = Res("fg")
            xi = [Slot(sbp("fxi%d" % i, [128, D], F32), Res("fxi"), "fxi%d" % i) for i in range(4)]
            yo = [Slot(sbp("fyo%d" % i, [128, D], F32), Res("fyo"), "fyo%d" % i) for i in range(3)]
            junk = sbp("fjunk", [128, D], BF16)
            r_junk = Res("junk")
            P.dma("sp", "fg", lambda e: e.dma_start(out=FG[:], in_=W["final_g"].rearrange("(o d) -> o d", o=1).partition_broadcast(128)), writes=[r_fg])
            for g in range(NSUB):
                s = xi[g % 4]
                o = yo[g % 3]
                ss, sq, rs = ssr.next(), sqr.next(), rsr.next()
                P.dma("sp", s.key, lambda e, s=s, g=g: e.dma_start(out=s.t[:], in_=src[g * 128:(g + 1) * 128, :]),
                      reads=[XR[g]] if src_is_X else [], writes=[s.res])
                P.act(lambda e, s=s, ss=ss: e.activation(out=junk[:], in_=s.t[:], func=AF.Square, accum_out=ss.t[:]),
                      reads=[s.res], writes=[r_junk, ss.res], noatt=True)
                P.act(lambda e, ss=ss, sq=sq: e.activation(out=sq.t[:], in_=ss.t[:], func=AF.Sqrt, bias=EPS, scale=1.0 / D),
                      reads=[ss.res], writes=[sq.res])
                P.dve(lambda e, sq=sq, rs=rs: e.reciprocal(out=rs.t[:], in_=sq.t[:]), reads=[sq.res], writes=[rs.res])
                P.dve(lambda e, s=s, o=o, rs=rs: e.scalar_tensor_tensor(out=o.t[:], in0=s.t[:], scalar=rs.t[:, 0:1], in1=FG[:],
                                                                        op0=ALU.mult, op1=ALU.mult), reads=[s.res, rs.res, r_fg], writes=[o.res])
                P.dma("pool", "st_" + o.key, lambda e, o=o, g=g: e.dma_start(out=yout[g * 128:(g + 1) * 128, :], in_=o.t[:]), reads=[o.res])

    first = True
    for l in layers:
        src, isx = (xin, False) if first else (X, True)
        first = False
        kind = l % 3
        if kind == 0:
            fnet_phase(l, src, isx)
        elif kind == 1:
            attn_phase(l, src, isx)
        else:
            sgu_phase(l, src, isx)
        ffn_phase(l)
    final_phase(X if not first else xin, not first)
    P.emit(nc)
    return nc, P


_CACHE = {}


def kernel(x_prompt, x_sample, c_prompt, c_sample, **w):
    ncores = 8
    Bp, Sp, _ = x_prompt.shape
    Bs, Ss, _ = x_sample.shape
    npc, nsc = Bp // ncores, Bs // ncores
    SEQS = tuple([Sp] * npc + [Ss] * nsc)
    key = SEQS
    if key not in _CACHE:
        _CACHE[key] = (build(SEQS)[0], _consts(SEQS))
    nc, consts = _CACHE[key]
    in_maps = []
    for c in range(ncores):
        xp = np.asarray(x_prompt[c * npc:(c + 1) * npc], np.float32).reshape(npc * Sp, D)
        xs = np.asarray(x_sample[c * nsc:(c + 1) * nsc], np.float32).reshape(nsc * Ss, D)
        m = {"xin": np.ascontiguousarray(np.concatenate([xp, xs], 0)),
             "cvec": np.ascontiguousarray(np.concatenate([np.asarray(c_prompt[c * npc:(c + 1) * npc], np.float32),
                                                          np.asarray(c_sample[c * nsc:(c + 1) * nsc], np.float32)], 0))}
        for n, _s in WSPECS:
            m[n] = np.ascontiguousarray(np.asarray(w[n], np.float32))
        m.update(consts)
        in_maps.append(m)
    res = run_bass_kernel_spmd(nc, in_maps, core_ids=list(range(ncores)))
    yp = np.empty((Bp, Sp, D), np.float32)
    ys = np.empty((Bs, Ss, D), np.float32)
    for c in range(ncores):
        y = np.asarray(res.results[c]["yout"])
        yp[c * npc:(c + 1) * npc] = y[:npc * Sp].reshape(npc, Sp, D)
        ys[c * nsc:(c + 1) * nsc] = y[npc * Sp:].reshape(nsc, Ss, D)
    return (yp, ys)
```

```python
import contextlib
import math
import numpy as np
import ml_dtypes
import concourse.bass as bass
import concourse.mybir as mybir
from concourse.bass_utils import run_bass_kernel_spmd

F32 = mybir.dt.float32
BF16 = mybir.dt.bfloat16
AF = mybir.ActivationFunctionType
ALU = mybir.AluOpType

D = 1024
KC = 8
FH = 2816
HC = 22
TOK = 512
NJ = 4
EPS = 1e-6
SEG = 12000
ATTACH_WAIT = True
COMPUTE = ("pe", "act", "dve", "pool")


class Res:
    __slots__ = ("name", "ws", "rs")

    def __init__(self, name=""):
        self.name = name
        self.ws = []
        self.rs = []


class Op:
    __slots__ = ("eng", "fn", "deps", "sig", "cnt", "key", "is_dma", "sem", "noatt")

    def __init__(self, eng, fn, is_dma, key):
        self.eng = eng
        self.fn = fn
        self.deps = []
        self.sig = is_dma
        self.cnt = 0
        self.key = key
        self.is_dma = is_dma
        self.sem = None
        self.noatt = False


def _covers(a, b):
    if a.is_dma != b.is_dma:
        return False
    if a.is_dma:
        return a.key == b.key
    return a.eng == b.eng


class Prog:
    def __init__(self):
        self.streams = {e: [] for e in ("pe", "act", "dve", "pool", "sp")}
        self.keyq = {}
        self.lastdma = {}
        self.lastcomp = {}
        self.pending = {}

    def _add(self, eng, fn, reads, writes, is_dma=False, key=None):
        op = Op(eng, fn, is_dma, key)
        deps = op.deps
        pend = self.pending.pop(eng, None)
        if pend:
            for p in pend:
                if (not p.is_dma) and (not is_dma) and p.eng == eng:
                    continue
                deps.append(p)
        if is_dma:
            assert self.keyq.setdefault(key, eng) == eng, key
            self.lastdma[key] = op
        else:
            self.lastcomp[eng] = op

        def dep(p, raw):
            if not p.is_dma and not is_dma and p.eng == eng:
                if eng == "pe" or not raw:
                    return
            deps.append(p)

        for r in reads:
            for w in r.ws:
                dep(w, True)
        for r in writes:
            if r.rs:
                for q in r.rs:
                    dep(q, False)
            else:
                for w in r.ws:
                    if w.is_dma and is_dma:
                        continue
                    dep(w, False)
        for r in writes:
            if r.rs:
                r.ws = [op]
                r.rs = []
            else:
                r.ws = [w for w in r.ws if not _covers(w, op)] + [op]
        for r in reads:
            if r in writes:
                continue
            r.rs = [q for q in r.rs if not _covers(q, op)] + [op]
        self.streams[eng].append(op)
        return op

    def pe(self, fn, reads=(), writes=()):
        return self._add("pe", fn, reads, writes)

    def act(self, fn, reads=(), writes=(), noatt=False):
        op = self._add("act", fn, reads, writes)
        op.noatt = noatt
        return op

    def dve(self, fn, reads=(), writes=()):
        return self._add("dve", fn, reads, writes)

    def pool(self, fn, reads=(), writes=()):
        return self._add("pool", fn, reads, writes)

    def dma(self, q, key, fn, reads=(), writes=()):
        return self._add(q, fn, reads, writes, is_dma=True, key=key)

    def barrier(self):
        marks = [op for op in self.lastcomp.values()] + list(self.lastdma.values())
        for e in self.streams:
            self.pending[e] = list(marks)

    def emit(self, nc):
        for s in self.streams.values():
            for op in s:
                for d in op.deps:
                    d.sig = True
        keycnt = {}
        engcnt = {e: 0 for e in COMPUTE}
        sem_names = []
        for e, s in self.streams.items():
            for op in s:
                if not op.sig:
                    continue
                if op.is_dma:
                    keycnt[op.key] = keycnt.get(op.key, 0) + 16
                    op.cnt = keycnt[op.key]
                    op.sem = "k_" + op.key
                else:
                    seg, c = divmod(engcnt[e], SEG)
                    engcnt[e] += 1
                    op.cnt = c + 1
                    op.sem = "%s_%d" % (e, seg)
                if op.sem not in sem_names:
                    sem_names.append(op.sem)
        self.nsem = len(sem_names)
        with contextlib.ExitStack() as st:
            sems = {n: st.enter_context(nc.semaphore(n)) for n in sem_names}
            block = st.enter_context(nc.Block())
            engmap = {"pe": block.tensor, "act": block.scalar, "dve": block.vector,
                      "pool": block.gpsimd, "sp": block.sync}
            for e, s in self.streams.items():
                def body(eng, s=s, e=e):
                    waited = {}
                    for op in s:
                        need = {}
                        for d in op.deps:
                            if d.cnt > need.get(d.sem, 0):
                                need[d.sem] = d.cnt
                        todo = [(sn, v) for sn, v in need.items() if waited.get(sn, 0) < v]
                        for sn, v in todo:
                            waited[sn] = v
                        att = None
                        if ATTACH_WAIT and todo and not op.is_dma and not op.noatt:
                            att = todo.pop()
                        for sn, v in todo:
                            eng.wait_ge(sems[sn], v)
                        ins = op.fn(eng)
                        if att is not None:
                            ins._wait_ge(sems[att[0]], att[1])
                        if op.sig:
                            ins.then_inc(sems[op.sem], 16 if op.is_dma else 1)
                    if e == "sp":
                        for k, v in keycnt.items():
                            eng.wait_ge(sems["k_" + k], v)
                engmap[e](body)


class Slot:
    __slots__ = ("t", "res", "key")

    def __init__(self, t, res, key):
        self.t = t
        self.res = res
        self.key = key


class Ring:
    def __init__(self, nc, name, n, shape, dtype):
        self.slots = [Slot(nc.alloc_sbuf_tensor("%s%d" % (name, i), list(shape), dtype), Res(name), "%s%d" % (name, i))
                      for i in range(n)]
        self.i = 0

    def next(self):
        s = self.slots[self.i % len(self.slots)]
        self.i += 1
        return s


def _consts(seq_lens):
    bf = ml_dtypes.bfloat16
    c = {}
    p = np.arange(128)
    n = np.arange(256)
    cd = np.zeros((128, 2, 512), np.float64)
    for kk in range(2):
        ch = kk * 128 + p
        ang = 2 * np.pi * ((ch[:, None] * n[None, :]) % 256) / 256.0
        cd[:, kk, :256] = np.cos(ang)
        cd[:, kk, 256:] = np.sin(ang)
    c["cd"] = cd.astype(bf)
    pm = np.where(p % 2 == 0, 1.0, -1.0)
    c["pm1"] = np.stack([pm, pm], 1).astype(bf)
    jp = np.zeros((128, 2, 2, 128), np.float32)
    for kold in range(2):
        for pp in range(128):
            cn = (256 - (kold * 128 + pp)) % 256
            jp[pp, kold, cn // 128, cn % 128] = 1.0
    c["jperm"] = jp.astype(bf)
    for S in sorted(set(seq_lens)):
        s = np.arange(S, dtype=np.int64)
        k = (s[:, None] * s[None, :]) % S
        ang = 2 * np.pi * k / float(S)
        cosm = np.cos(ang).astype(bf)
        msin = (-np.sin(ang)).astype(bf)
        both = np.stack([cosm, msin], 0).reshape(2, S // 128, 128, S // 512, 512)
        c["ds%d" % S] = np.ascontiguousarray(both.transpose(3, 2, 1, 0, 4))
    t = np.arange(4096)
    row = (t // 64).astype(np.float32)
    col = (t % 64).astype(np.float32)
    freqs = (1.0 / (np.float32(10000.0) ** (np.arange(0, 64, 2, dtype=np.float32) / np.float32(64)))).astype(np.float32)
    cosT = np.zeros((128, 4096), np.float32)
    sinT = np.zeros((128, 4096), np.float32)
    rm = np.zeros((128, 128), np.float32)
    for d in range(128):
        pos = row if d < 64 else col
        ang = (pos * freqs[d % 32]).astype(np.float32)
        cosT[d] = np.cos(ang)
        sinT[d] = np.sin(ang)
        if d % 64 < 32:
            rm[d, d + 32] = -1.0
        else:
            rm[d, d - 32] = 1.0
    c["ropec"] = cosT
    c["ropes"] = sinT
    c["rmT"] = np.ascontiguousarray(rm.T)
    return c


WSPECS = [("norm1_g", [4, D]), ("norm2_g", [4, D]), ("w_ada", [4, D, 6 * D]), ("b_ada", [4, 6 * D]),
          ("fnet_w_o", [2, D, D]), ("attn_w_qkv", [1, D, 1536]), ("attn_q_g", [1, 128]), ("attn_k_g", [1, 128]),
          ("attn_w_o", [1, D, D]), ("sgu_w_in", [1, D, 2 * D]), ("sgu_ln_g", [1, D]), ("sgu_ln_b", [1, D]),
          ("sgu_w_s", [1, 8, 128, 128]), ("sgu_b_s", [1, 8, 128]), ("sgu_w_o", [1, D, D]),
          ("ffn_w_gu", [4, D, 2 * FH]), ("ffn_w_down", [4, FH, D]), ("final_g", [D])]


def build(SEQS, layers=(0, 1, 2, 3)):
    nc = bass.Bass("TRN2", target_bir_lowering=False)
    NB = len(SEQS)
    NT = sum(SEQS)
    NSUB = NT // 128
    offs = [sum(SEQS[:i]) for i in range(NB)]
    SMAX = max(SEQS)

    def din(name, shape, dt=F32):
        return nc.dram_tensor(name, list(shape), dt, kind="ExternalInput").ap()

    xin = din("xin", [NT, D])
    cvec = din("cvec", [NB, D])
    W = {n: din(n, s) for n, s in WSPECS}
    cd_d = din("cd", [128, 2, 512], BF16)
    pm1_d = din("pm1", [128, 2], BF16)
    jperm_d = din("jperm", [128, 2, 2, 128], BF16)
    ds_d = {S: din("ds%d" % S, [S // 512, 128, S // 128, 2, 512], BF16) for S in sorted(set(SEQS))}
    ropec_d = din("ropec", [128, 4096])
    ropes_d = din("ropes", [128, 4096])
    rmT_d = din("rmT", [128, 128])
    yout = nc.dram_tensor("yout", [NT, D], F32, kind="ExternalOutput").ap()
    X = nc.dram_tensor("Xs", [NT, D], F32, kind="Internal").ap()
    PQ = nc.dram_tensor("PQs", [NT, 2048], BF16, kind="Internal").ap()
    GT = nc.dram_tensor("GTs", [4, 2, NB, 128, D], F32, kind="Internal").ap()

    P = Prog()
    uid = [0]
    XR = [Res("X%d" % i) for i in range(NSUB)]
    PQR = [Res("PQ%d" % i) for i in range(NSUB)]
    GTR = Res("GT")

    def sb(name, shape, dt):
        return nc.alloc_sbuf_tensor(name, list(shape), dt)

    identf = sb("identf", [128, 128], F32)
    ident = sb("ident", [128, 128], BF16)
    onesb = sb("onesb", [128, 128], BF16)
    AA = sb("AA", [128, 4, 2, KC, NB], F32)
    BB = sb("BB", [128, 4, 2, KC, NB], F32)
    fgT = sb("fgT", [128, KC], F32)
    r_const = Res("const")
    r_AB = Res("AB")
    pb = [nc.alloc_psum_tensor("pb%d" % i, [128, 512], F32) for i in range(8)]
    bres = [Res("pb%d" % i) for i in range(8)]

    def pbT(i):
        return pb[i][:].bitcast(BF16).rearrange("p (a n) -> p a n", n=128)

    P.pool(lambda e: e.memset(identf[:], 0.0), writes=[r_const])
    P.pool(lambda e: e.affine_select(out=identf[:], in_=identf[:], pattern=[[-1, 128]], compare_op=ALU.not_equal,
                                     fill=1.0, base=0, channel_multiplier=1), reads=[r_const], writes=[r_const])
    P.pool(lambda e: e.memset(onesb[:], 1.0 / 128.0), writes=[r_const])
    P.dve(lambda e: e.tensor_copy(out=ident[:], in_=identf[:]), reads=[r_const], writes=[r_const])

    ssr = Ring(nc, "ssr", 4, [128, 1], F32)
    sqr = Ring(nc, "sqr", 4, [128, 1], F32)
    rsr = Ring(nc, "rsr", 4, [128, 1], F32)

    def tr_rows(src_ap, R, dst_ap, bank, func=AF.Copy, stage=None):
        P.dma("sp", "trs", lambda e: e.dma_start(out=stage.t[0:R, :], in_=src_ap), writes=[stage.res])
        P.pe(lambda e: e.transpose(out=pb[bank][:, 0:R], in_=stage.t[0:R, :], identity=identf[0:R, 0:R]),
             reads=[stage.res, r_const], writes=[bres[bank]])
        P.act(lambda e: e.activation(out=dst_ap, in_=pb[bank][:, 0:R], func=func), reads=[bres[bank]], writes=[r_AB])

    with contextlib.ExitStack() as ph:
        def sbp(name, shape, dt):
            return ph.enter_context(nc.sbuf_tensor(name, list(shape), dt))
        stage = Slot(sbp("stage", [128, 128], F32), Res("stage"), "trs")
        csT = sbp("csT", [128, NB, KC], F32)
        csT2 = sbp("csT2", [128, KC, NB], F32)
        csrep = sbp("csrep", [128, NB, KC, 128], F32)
        n1T = sbp("n1T", [128, 4, KC], F32)
        n2T = sbp("n2T", [128, 4, KC], F32)
        baT = sbp("baT", [128, 192], F32)
        modT = sbp("modT", [128, 4, 48, NB], F32)
        wring = [Slot(sbp("wblk%d" % i, [128, KC, 512], F32), Res("wblk"), "wblk%d" % i) for i in range(3)]
        wbring = [Slot(sbp("wbb%d" % i, [128, KC, 512], BF16), Res("wbb"), "wbb%d" % i) for i in range(3)]
        csrepb = sbp("csrepb", [128, NB, KC, 128], BF16)
        ngb = 0
        brow = [Slot(sbp("brow%d" % i, [128, 512], F32), Res("brow"), "brow%d" % i) for i in range(2)]
        grow = [Slot(sbp("grow%d" % i, [128, 512], F32), Res("grow"), "grow%d" % i) for i in range(2)]
        r_cs = Res("cs")
        r_mod = Res("mod")

        tr_rows(cvec.rearrange("b (k p) -> (b k) p", p=128), NB * KC, csT[:].rearrange("p b k -> p (b k)"), 0, AF.Silu, stage)
        tr_rows(W["norm1_g"].rearrange("l (k p) -> (l k) p", p=128), 32, n1T[:].rearrange("p l k -> p (l k)"), 1, AF.Copy, stage)
        tr_rows(W["norm2_g"].rearrange("l (k p) -> (l k) p", p=128), 32, n2T[:].rearrange("p l k -> p (l k)"), 0, AF.Copy, stage)
        tr_rows(W["final_g"].rearrange("(k p) -> k p", p=128), 8, fgT[:], 1, AF.Copy, stage)
        bav = W["b_ada"].rearrange("l (c p) -> (l c) p", p=128)
        tr_rows(bav[0:96, :], 96, baT[:, 0:96], 0, AF.Copy, stage)
        tr_rows(bav[96:192, :], 96, baT[:, 96:192], 1, AF.Copy, stage)
        P.dve(lambda e: e.tensor_copy(out=csT2[:], in_=csT[:].rearrange("p b k -> p k b")), reads=[r_AB], writes=[r_cs])
        P.dve(lambda e: e.tensor_copy(out=csrep[:].rearrange("p b k n -> p (b k) n"),
                                      in_=csT[:].rearrange("p b (k o) -> p (b k) o", o=1).broadcast_to([128, NB * KC, 128])),
              reads=[r_AB], writes=[r_cs])
        P.dve(lambda e: e.tensor_copy(out=csrepb[:], in_=csrep[:]), reads=[r_cs], writes=[r_cs])
        nblk = 0
        ngate = 0
        for l in layers:
            for cb in range(12):
                region = cb // 2
                if region in (2, 5):
                    ws = wbring[ngb % 3]
                    ngb += 1
                    P.dma("pool", ws.key, lambda e, ws=ws, l=l, cb=cb: e.dma_start(
                        out=ws.t[:], in_=W["w_ada"][l, :, cb * 512:(cb + 1) * 512].rearrange("(k p) n -> p k n", p=128)),
                        writes=[ws.res])
                else:
                    ws = wring[nblk % 3]
                    nblk += 1
                    P.dma("sp", ws.key, lambda e, ws=ws, l=l, cb=cb: e.dma_start(
                        out=ws.t[:], in_=W["w_ada"][l, :, cb * 512:(cb + 1) * 512].rearrange("(k p) n -> p k n", p=128)),
                        writes=[ws.res])
                if region in (2, 5):
                    which = 0 if region == 2 else 1
                    half = cb % 2
                    br = brow[ngate % 2]
                    P.dma("sp", br.key, lambda e, br=br, l=l, cb=cb: e.dma_start(
                        out=br.t[:], in_=W["b_ada"][l:l + 1, cb * 512:(cb + 1) * 512].partition_broadcast(128)), writes=[br.res])
                    for b in range(NB):
                        bank = 2 + (b % 2)
                        for k in range(KC):
                            P.pe(lambda e, b=b, k=k, bank=bank, ws=ws: e.matmul(
                                pb[bank][:], lhsT=csrepb[:, b, k, :], rhs=ws.t[:, k, :], start=(k == 0), stop=(k == KC - 1)),
                                reads=[r_cs, ws.res], writes=[bres[bank]])
                        gr = grow[ngate % 2]
                        ngate += 1
                        P.dve(lambda e, gr=gr, bank=bank, br=br: e.tensor_tensor(out=gr.t[:], in0=pb[bank][:], in1=br.t[:], op=ALU.add),
                              reads=[bres[bank], br.res], writes=[gr.res])
                        P.dma("sp", gr.key, lambda e, gr=gr, l=l, which=which, b=b, half=half: e.dma_start(
                            out=GT[l, which, b, :, half * 512:(half + 1) * 512], in_=gr.t[:]), reads=[gr.res], writes=[GTR])
                else:
                    bank = 4 + (nblk % 2)
                    for oc in range(4):
                        for k in range(KC):
                            P.pe(lambda e, oc=oc, k=k, bank=bank, ws=ws: e.matmul(
                                pb[bank][:, oc * NB:(oc + 1) * NB], lhsT=ws.t[:, k, oc * 128:(oc + 1) * 128], rhs=csT2[:, k, :],
                                start=(k == 0), stop=(k == KC - 1)), reads=[r_cs, ws.res], writes=[bres[bank]])
                    ch0 = cb * 4
                    P.dve(lambda e, bank=bank, l=l, ch0=ch0: e.tensor_tensor(
                        out=modT[:, l, ch0:ch0 + 4, :], in0=pb[bank][:, 0:4 * NB].rearrange("p (c b) -> p c b", b=NB),
                        in1=baT[:, l * 48 + ch0:l * 48 + ch0 + 4].rearrange("p (c o) -> p c o", o=1).broadcast_to([128, 4, NB]),
                        op=ALU.add), reads=[bres[bank], r_AB], writes=[r_mod])
            for w_, (nT, sc0, sh0) in enumerate(((n1T, 8, 0), (n2T, 32, 24))):
                P.dve(lambda e, l=l, w_=w_, nT=nT, sc0=sc0: e.scalar_tensor_tensor(
                    out=AA[:, l, w_, :, :], in0=modT[:, l, sc0:sc0 + 8, :], scalar=1.0,
                    in1=nT[:, l, :].rearrange("p (k o) -> p k o", o=1).broadcast_to([128, KC, NB]),
                    op0=ALU.add, op1=ALU.mult), reads=[r_mod, r_AB], writes=[r_AB])
                P.dve(lambda e, l=l, w_=w_, sh0=sh0: e.tensor_copy(out=BB[:, l, w_, :, :], in_=modT[:, l, sh0:sh0 + 8, :]),
                      reads=[r_mod], writes=[r_AB])
        P.barrier()

    def seq_of_tile(ti):
        t0 = ti * TOK
        for b in range(NB):
            if offs[b] <= t0 < offs[b] + SEQS[b]:
                return b
        raise AssertionError

    class Tools:
        pass

    def make_tools(ph, nxa=3, nxr=3, nhT=1, nxn=2, batched=False):
        def sbp(name, shape, dt):
            uid[0] += 1
            return ph.enter_context(nc.sbuf_tensor("%s_%d" % (name, uid[0]), list(shape), dt))
        T = Tools()
        T.sbp = sbp
        T.xa = [Slot(sbp("xa%d" % i, [128, D], F32), Res("xa"), "xa%d" % i) for i in range(nxa)]
        T.xr = [Slot(sbp("xr%d" % i, [128, D], F32), Res("xr"), "xr%d" % i) for i in range(nxr)]
        if nxa:
            T.xn = [Slot(sbp("xn%d" % i, [128, D], BF16), Res("xn"), None) for i in range(nxn)]
            T.hTs = [Slot(sbp("hT%d" % i, [128, KC, TOK], BF16), Res("hT"), None) for i in range(nhT)]
        if nxr:
            T.tmp = [Slot(sbp("tmp%d" % i, [128, 512], F32), Res("tmp"), None) for i in range(2)]
            T.G = [Slot(sbp("G%d" % i, [128, D], F32), Res("G"), "G%d" % i) for i in range(1)]
        T.cnt = {"xn": 0, "tmp": 0, "G": 0, "tb": 0, "hT": 0, "ss4": 0}
        T.batched = batched
        if batched:
            T.ss4 = [Slot(sbp("ss4%d" % i, [128, 12], F32), Res("ss4"), None) for i in range(3)]
        return T

    def _nl_emit(T):
        src_, isx = T.nsrc
        pos = T.nl_e
        g = T.norder[pos]
        s = T.xa[pos % len(T.xa)]
        P.dma("sp", s.key, lambda e, s=s, g=g: e.dma_start(out=s.t[:], in_=src_[g * 128:(g + 1) * 128, :]),
              reads=[XR[g]] if isx else [], writes=[s.res])
        T.nl_e += 1

    def norm_setup(T, order, src_, isx, l=None, w_=None, tbanks=(0, 1)):
        T.ntiles_order = list(order)
        T.ntile_i = 0
        T.nparams = (l, w_, tbanks)
        T.norder = [ti * NJ + j for ti in T.ntiles_order for j in range(NJ)]
        T.nsrc = (src_, isx)
        T.nl_e = 0
        T.npos = 0
        for _ in range(min(len(T.xa), len(T.norder))):
            _nl_emit(T)

    def norm_next(T, split=False):
        if T.ntile_i >= len(T.ntiles_order):
            return (None, [], []) if split else None
        ti = T.ntiles_order[T.ntile_i]
        T.ntile_i += 1
        l, w_, tbanks = T.nparams
        hs = T.hTs[T.cnt["hT"] % len(T.hTs)]
        T.cnt["hT"] += 1
        if getattr(T, "batched", False):
            A, B = norm_parts_batched(T, ti, l, w_, tbanks, hs)
        else:
            A, B = norm_parts(T, ti, l, w_, tbanks, hs)
        if split:
            return hs, A, B
        for f_ in A:
            f_()
        for f_ in B:
            f_()
        return hs

    def norm_parts(T, ti, l, w_, tbanks, hs):
        b = seq_of_tile(ti)
        st = {}

        def A(j):
            assert T.norder[T.npos] == ti * NJ + j
            s = T.xa[T.npos % len(T.xa)]
            T.npos += 1
            xn = T.xn[T.cnt["xn"] % len(T.xn)]
            T.cnt["xn"] += 1
            st[j] = xn
            ss = ssr.next()
            sq = sqr.next()
            rs = rsr.next()
            P.act(lambda e, s=s, xn=xn, ss=ss: e.activation(out=xn.t[:], in_=s.t[:], func=AF.Square, accum_out=ss.t[:]),
                  reads=[s.res], writes=[xn.res, ss.res], noatt=True)
            P.act(lambda e, ss=ss, sq=sq: e.activation(out=sq.t[:], in_=ss.t[:], func=AF.Sqrt, bias=EPS, scale=1.0 / D),
                  reads=[ss.res], writes=[sq.res])
            P.dve(lambda e, sq=sq, rs=rs: e.reciprocal(out=rs.t[:], in_=sq.t[:]), reads=[sq.res], writes=[rs.res])
            P.act(lambda e, s=s, xn=xn, rs=rs: e.activation(out=xn.t[:], in_=s.t[:], func=AF.Copy, scale=rs.t[:, 0:1]),
                  reads=[s.res, rs.res], writes=[xn.res])
            if T.nl_e < len(T.norder):
                _nl_emit(T)

        def B(j):
            xn = st[j]
            bank = tbanks[T.cnt["tb"] % len(tbanks)]
            T.cnt["tb"] += 1
            for k in range(KC):
                P.pe(lambda e, xn=xn, k=k, bank=bank: e.transpose(out=pbT(bank)[:, k, :], in_=xn.t[:, k * 128:(k + 1) * 128],
                                                                  identity=ident[:]),
                     reads=[xn.res, r_const], writes=[bres[bank]])
            for k in range(KC):
                P.dve(lambda e, k=k, j=j, bank=bank, l=l, w_=w_, b=b, hs=hs: e.tensor_scalar(
                    out=hs.t[:, k, j * 128:(j + 1) * 128], in0=pbT(bank)[:, k, :],
                    scalar1=AA[:, l, w_, k, b:b + 1], scalar2=BB[:, l, w_, k, b:b + 1], op0=ALU.mult, op1=ALU.add),
                    reads=[bres[bank], r_AB], writes=[hs.res])

        return [lambda j=j: A(j) for j in range(NJ)], [lambda j=j: B(j) for j in range(NJ)]

    def norm_parts_batched(T, ti, l, w_, tbanks, hs):
        b = seq_of_tile(ti)
        st = {}
        ss4 = T.ss4[T.cnt["ss4"] % len(T.ss4)]
        T.cnt["ss4"] += 1

        sl = []

        def A1():
            for j in range(NJ):
                assert T.norder[T.npos] == ti * NJ + j
                s = T.xa[T.npos % len(T.xa)]
                T.npos += 1
                xn = T.xn[T.cnt["xn"] % len(T.xn)]
                T.cnt["xn"] += 1
                st[j] = xn
                sl.append(s)
                P.act(lambda e, s=s, xn=xn, j=j: e.activation(out=xn.t[:], in_=s.t[:], func=AF.Square, accum_out=ss4.t[:, j:j + 1]),
                      reads=[s.res], writes=[xn.res, ss4.res], noatt=True)
            P.act(lambda e: e.activation(out=ss4.t[:, 4:8], in_=ss4.t[:, 0:4], func=AF.Sqrt, bias=EPS, scale=1.0 / D),
                  reads=[ss4.res], writes=[ss4.res])

        def A2():
            P.dve(lambda e: e.reciprocal(out=ss4.t[:, 8:12], in_=ss4.t[:, 4:8]), reads=[ss4.res], writes=[ss4.res])
            for j in range(NJ):
                s, xn = sl[j], st[j]
                P.act(lambda e, s=s, xn=xn, j=j: e.activation(out=xn.t[:], in_=s.t[:], func=AF.Copy, scale=ss4.t[:, 8 + j:9 + j]),
                      reads=[s.res, ss4.res], writes=[xn.res])
                if T.nl_e < len(T.norder):
                    _nl_emit(T)

        def B(j):
            xn = st[j]
            bank = tbanks[T.cnt["tb"] % len(tbanks)]
            T.cnt["tb"] += 1
            for k in range(KC):
                P.pe(lambda e, xn=xn, k=k, bank=bank: e.transpose(out=pbT(bank)[:, k, :], in_=xn.t[:, k * 128:(k + 1) * 128],
                                                                  identity=ident[:]),
                     reads=[xn.res, r_const], writes=[bres[bank]])
            for k in range(KC):
                P.dve(lambda e, k=k, j=j, bank=bank, l=l, w_=w_, b=b, hs=hs: e.tensor_scalar(
                    out=hs.t[:, k, j * 128:(j + 1) * 128], in0=pbT(bank)[:, k, :],
                    scalar1=AA[:, l, w_, k, b:b + 1], scalar2=BB[:, l, w_, k, b:b + 1], op0=ALU.mult, op1=ALU.add),
                    reads=[bres[bank], r_AB], writes=[hs.res])

        return [A1, A2], [lambda j=j: B(j) for j in range(NJ)]

    def load_G(T, l, which, b):
        s = T.G[T.cnt["G"] % len(T.G)]
        T.cnt["G"] += 1
        P.dma("sp", s.key, lambda e, s=s: e.dma_start(out=s.t[:], in_=GT[l, which, b, :, :]), reads=[GTR], writes=[s.res])
        return s

    def _rl_emit(T):
        src_, isx = T.rsrc
        pos = T.rl_e
        g = T.rorder[pos]
        s = T.xr[pos % len(T.xr)]
        P.dma("sp", s.key, lambda e, s=s, g=g: e.dma_start(out=s.t[:], in_=src_[g * 128:(g + 1) * 128, :]),
              reads=[XR[g]] if isx else [], writes=[s.res])
        T.rl_e += 1

    def resid_setup(T, order, src_, isx):
        T.rorder = [ti * NJ + j for ti in order for j in range(NJ)]
        T.rsrc = (src_, isx)
        T.rl_e = 0
        T.rpos = 0
        for _ in range(min(len(T.xr), len(T.rorder))):
            _rl_emit(T)

    def resid_apply(T, ti, j, nh, yb, G):
        assert T.rorder[T.rpos] == ti * NJ + j
        s = T.xr[T.rpos % len(T.xr)]
        tmp = T.tmp[T.cnt["tmp"] % 2]
        T.cnt["tmp"] += 1
        P.dve(lambda e, tmp=tmp, yb=yb, G=G, nh=nh: e.tensor_tensor(out=tmp.t[:], in0=pb[yb][:], in1=G.t[:, nh * 512:(nh + 1) * 512],
                                                                    op=ALU.mult), reads=[bres[yb], G.res], writes=[tmp.res])
        P.pool(lambda e, s=s, tmp=tmp, nh=nh: e.tensor_tensor(out=s.t[:, nh * 512:(nh + 1) * 512], in0=s.t[:, nh * 512:(nh + 1) * 512],
                                                              in1=tmp.t[:], op=ALU.add), reads=[tmp.res, s.res], writes=[s.res])

    def resid_store(T, ti, j, dst=None):
        s = T.xr[T.rpos % len(T.xr)]
        T.rpos += 1
        g = ti * NJ + j
        P.dma("pool", "st_" + s.key, lambda e, s=s, g=g: e.dma_start(out=X[g * 128:(g + 1) * 128, :], in_=s.t[:]),
              reads=[s.res], writes=[XR[g]])
        if T.rl_e < len(T.rorder):
            _rl_emit(T)

    def load_w_cast(dst, dst_res, src2d, key, nchunk_rows, ncols, colsplit):
        cw = ncols // colsplit
        for k in range(nchunk_rows):
            for c in range(colsplit):
                P.dma("pool", key, lambda e, k=k, c=c: e.dma_start(
                    out=dst[:, k, c * cw:(c + 1) * cw], in_=src2d[k * 128:(k + 1) * 128, c * cw:(c + 1) * cw]), writes=[dst_res])

    ntiles = NT // TOK

    def ffn_phase(l):
        with contextlib.ExitStack() as ph:
            T = make_tools(ph, nxa=3, nxr=2, nhT=1, nxn=4)
            Wgu = T.sbp("Wgu", [128, KC, 2 * FH], BF16)
            Wd = T.sbp("Wd", [128, HC, D], BF16)
            aT = T.sbp("aT", [128, HC, TOK], BF16)
            sg = [Slot(T.sbp("sg%d" % i, [128, TOK], BF16), Res("sg"), None) for i in range(2)]
            r_wd, r_aT = Res("wd"), Res("aT")
            r_wgs = [Res("wgu%d" % i) for i in range(4)]
            cw = 2 * FH // 4
            wsrc = W["ffn_w_gu"][l].rearrange("(k p) n -> p k n", p=128)
            for c_ in (0, 2, 1, 3):
                P.dma("pool", "wgu%d" % c_, lambda e, c_=c_: e.dma_start(out=Wgu[:, :, c_ * cw:(c_ + 1) * cw], in_=wsrc[:, :, c_ * cw:(c_ + 1) * cw]),
                      writes=[r_wgs[c_]])
            load_w_cast(Wd, r_wd, W["ffn_w_down"][l], "wd", HC, D, 1)
            norm_setup(T, range(ntiles), X, True, l, 1, (0, 1))
            resid_setup(T, range(ntiles), X, True)
            cur = norm_next(T)
            G = None
            gb_ = -1
            for ti in range(ntiles):
                b = seq_of_tile(ti)
                if b != gb_:
                    G = load_G(T, l, 1, b)
                    gb_ = b
                nxt, An, Bn = norm_next(T, split=True)
                for hc in range(HC):
                    if An and hc in (3, 8, 13, 18):
                        An[(hc - 3) // 5]()
                    pr = hc % 2
                    gbk, ubk = 2 + 2 * pr, 3 + 2 * pr
                    for k in range(KC):
                        P.pe(lambda e, k=k, hc=hc, gbk=gbk, cur=cur: e.matmul(pb[gbk][:], lhsT=Wgu[:, k, hc * 128:(hc + 1) * 128], rhs=cur.t[:, k, :],
                                                                      start=(k == 0), stop=(k == KC - 1)),
                             reads=[r_wgs[0 if hc < 11 else 1], cur.res], writes=[bres[gbk]])
                    for k in range(KC):
                        P.pe(lambda e, k=k, hc=hc, ubk=ubk, cur=cur: e.matmul(pb[ubk][:], lhsT=Wgu[:, k, FH + hc * 128:FH + (hc + 1) * 128],
                                                                      rhs=cur.t[:, k, :], start=(k == 0), stop=(k == KC - 1)),
                             reads=[r_wgs[2 if hc < 11 else 3], cur.res], writes=[bres[ubk]])
                    s_ = sg[hc % 2]
                    P.act(lambda e, s_=s_, gbk=gbk: e.activation(out=s_.t[:], in_=pb[gbk][:], func=AF.Silu),
                          reads=[bres[gbk]], writes=[s_.res])
                    P.dve(lambda e, s_=s_, ubk=ubk, hc=hc: e.tensor_tensor(out=aT[:, hc, :], in0=pb[ubk][:], in1=s_.t[:], op=ALU.mult),
                          reads=[bres[ubk], s_.res], writes=[r_aT])
                for f_ in Bn:
                    f_()
                cur = nxt
                for j in range(NJ):
                    for nh in range(2):
                        yb = 6 + nh
                        for hc in range(HC):
                            P.pe(lambda e, hc=hc, j=j, nh=nh, yb=yb: e.matmul(
                                pb[yb][:], lhsT=aT[:, hc, j * 128:(j + 1) * 128], rhs=Wd[:, hc, nh * 512:(nh + 1) * 512],
                                start=(hc == 0), stop=(hc == HC - 1)), reads=[r_aT, r_wd], writes=[bres[yb]])
                        resid_apply(T, ti, j, nh, yb, G)
                    resid_store(T, ti, j)
            P.barrier()

    def fnet_phase(l, src, src_is_X):
        jw = l // 3
        with contextlib.ExitStack() as ph:
            T = make_tools(ph, nxa=8, nxr=0, nhT=2, nxn=8, batched=True)
            cdt = T.sbp("cdt", [128, 2, 512], BF16)
            r_cd = Res("cd")
            pqs = [Slot(T.sbp("pqs%d" % i, [128, 2048], BF16), Res("pqs"), "pqs%d" % i) for i in range(4)]
            P.dma("sp", "cd", lambda e: e.dma_start(out=cdt[:], in_=cd_d), writes=[r_cd])
            norm_setup(T, range(ntiles), src, src_is_X, l, 0, (0, 1))
            npq = 0
            cur = norm_next(T)
            for ti in range(ntiles):
                nxt, An, Bn = norm_next(T, split=True)
                if An:
                    An[0]()
                for j in range(NJ):
                    if j == 2 and An:
                        An[1]()
                    ps = pqs[npq % 4]
                    npq += 1
                    for g in range(4):
                        bank = 2 + (npq * 4 + g) % 6
                        for kk in range(2):
                            P.pe(lambda e, g=g, kk=kk, j=j, bank=bank, cur=cur: e.matmul(
                                pb[bank][:], lhsT=cur.t[:, 2 * g + kk, j * 128:(j + 1) * 128], rhs=cdt[:, kk, :],
                                start=(kk == 0), stop=(kk == 1)), reads=[cur.res, r_cd], writes=[bres[bank]])
                        if g == 3:
                            P.act(lambda e, ps=ps, g=g, bank=bank: e.activation(out=ps.t[:, g * 512:(g + 1) * 512], in_=pb[bank][:], func=AF.Copy),
                                  reads=[bres[bank]], writes=[ps.res])
                        else:
                            P.dve(lambda e, ps=ps, g=g, bank=bank: e.tensor_copy(out=ps.t[:, g * 512:(g + 1) * 512], in_=pb[bank][:]),
                                  reads=[bres[bank]], writes=[ps.res])
                    gs = ti * NJ + j
                    P.dma("pool", ps.key, lambda e, ps=ps, gs=gs: e.dma_start(out=PQ[gs * 128:(gs + 1) * 128, :], in_=ps.t[:]),
                          reads=[ps.res], writes=[PQR[gs]])
                for f_ in Bn:
                    f_()
                cur = nxt
            P.barrier()
        with contextlib.ExitStack() as ph:
            T2 = make_tools(ph, nxa=0, nxr=3)
            resid_setup(T2, range(ntiles), src, src_is_X)
            NSC = SMAX // 128
            PQh = T2.sbp("PQh", [128, NSC, 1024], BF16)
            FT = T2.sbp("FT", [128, KC, SMAX], BF16)
            Wfo = T2.sbp("Wfo", [128, KC, D], BF16)
            slab = [Slot(T2.sbp("slab%d" % i, [128, 4, 2, 512], BF16), Res("slab"), "slab%d" % i) for i in range(4)]
            r_ft, r_wfo, r_ftB, r_fc = Res("ft"), Res("wfo"), Res("ftB"), Res("fc")
            r_pqg = [Res("pqh%d" % i) for i in range(NSC // 4)]
            pm1 = T2.sbp("pm1", [128, 2], BF16)
            jp = T2.sbp("jp", [128, 2, 2, 128], BF16)
            P.dma("sp", "fc", lambda e: e.dma_start(out=pm1[:], in_=pm1_d), writes=[r_fc])
            P.dma("sp", "fc", lambda e: e.dma_start(out=jp[:], in_=jperm_d), writes=[r_fc])
            nmb = 0
            load_w_cast(Wfo, r_wfo, W["fnet_w_o"][jw], "wfo", KC, D, 1)
            nslab = 0
            nacc = 0
            for b in range(NB):
                S = SEQS[b]
                nsc = S // 128
                nbt = S // 512
                sub0 = offs[b] // 128
                scale = 1.0 / math.sqrt(S * 256.0)
                herm = (nbt % 2 == 0)
                nbd = nbt // 2 if herm else nbt
                for hf in range(2):
                    for q in range(nsc // 4):
                        P.dma("sp", "pqh%d" % q, lambda e, q=q, hf=hf, sub0=sub0: e.dma_start(
                            out=PQh[:, q * 4:(q + 1) * 4, :],
                            in_=PQ[(sub0 + q * 4) * 128:(sub0 + q * 4 + 4) * 128, hf * 1024:(hf + 1) * 1024].rearrange("(c p) n -> p c n", p=128)),
                            reads=[PQR[sub0 + q * 4 + i] for i in range(4)], writes=[r_pqg[q]])
                    for bt in range(nbd):
                        base = 4 * (nacc % 2)
                        nacc += 1
                        for q in range(nsc // 4):
                            sl = slab[nslab % 4]
                            nslab += 1
                            P.dma("sp", sl.key, lambda e, sl=sl, S=S, bt=bt, q=q: e.dma_start(
                                out=sl.t[:], in_=ds_d[S][bt, :, q * 4:(q + 1) * 4, :, :]), writes=[sl.res])
                            for s4 in range(4):
                                sc = q * 4 + s4
                                for cc in range(4):
                                    gl = cc // 2
                                    c0 = gl * 512 + (cc % 2) * 128
                                    P.pe(lambda e, sl=sl, s4=s4, sc=sc, cc=cc, c0=c0, base=base: e.matmul(
                                        pb[base + cc][:], lhsT=PQh[:, sc, c0:c0 + 128], rhs=sl.t[:, s4, 0, :], start=(sc == 0), stop=False),
                                        reads=[r_pqg[sc // 4], sl.res], writes=[bres[base + cc]])
                                    P.pe(lambda e, sl=sl, s4=s4, sc=sc, cc=cc, c0=c0, base=base, nsc=nsc: e.matmul(
                                        pb[base + cc][:], lhsT=PQh[:, sc, c0 + 256:c0 + 384], rhs=sl.t[:, s4, 1, :], start=False,
                                        stop=(sc == nsc - 1)), reads=[r_pqg[sc // 4], sl.res], writes=[bres[base + cc]])
                        for cc in range(4):
                            fo = FT[:, hf * 4 + cc, bt * 512:(bt + 1) * 512]
                            if cc % 2 == 0:
                                P.act(lambda e, fo=fo, base=base, cc=cc, scale=scale: e.activation(out=fo, in_=pb[base + cc][:], func=AF.Copy, scale=scale),
                                      reads=[bres[base + cc]], writes=[r_ft])
                            else:
                                P.dve(lambda e, fo=fo, base=base, cc=cc, scale=scale: e.tensor_scalar(out=fo, in0=pb[base + cc][:], scalar1=scale, scalar2=None, op0=ALU.mult),
                                      reads=[bres[base + cc]], writes=[r_ft])
                    if herm:
                        base = 4 * (nacc % 2)
                        nacc += 1
                        for cc in range(4):
                            c0 = (cc // 2) * 512 + (cc % 2) * 128
                            for sc in range(nsc):
                                P.pe(lambda e, sc=sc, cc=cc, c0=c0, base=base, nsc=nsc: e.matmul(
                                    pb[base][:, 2 * cc:2 * cc + 2], lhsT=PQh[:, sc, c0:c0 + 128], rhs=pm1[:, 0:2], start=(sc == 0),
                                    stop=(sc == nsc - 1)), reads=[r_pqg[sc // 4], r_fc], writes=[bres[base]])
                            P.dve(lambda e, cc=cc, base=base, hf=hf, S=S, scale=scale: e.tensor_scalar(
                                out=FT[:, hf * 4 + cc, S // 2:S // 2 + 1], in0=pb[base][:, 2 * cc:2 * cc + 1], scalar1=scale, scalar2=None,
                                op0=ALU.mult), reads=[bres[base]], writes=[r_ft])
                if herm:
                    for g in range(4):
                        for m in range(nbd):
                            o0 = S // 2 + 512 * m
                            a0 = S // 2 - 512 * m - 511
                            for knew in range(2):
                                bank = nmb % 8
                                nmb += 1
                                for kold in range(2):
                                    P.pe(lambda e, g=g, kold=kold, knew=knew, a0=a0, bank=bank: e.matmul(
                                        pb[bank][:], lhsT=jp[:, kold, knew, :], rhs=FT[:, 2 * g + kold, a0:a0 + 512], start=(kold == 0),
                                        stop=(kold == 1)), reads=[r_ft, r_fc], writes=[bres[bank]])
                                if m == 0:
                                    fo = FT[:, 2 * g + knew, o0 + 1:o0 + 512]
                                    fi = pb[bank][:, 0:511][:, ::-1]
                                else:
                                    fo = FT[:, 2 * g + knew, o0:o0 + 512]
                                    fi = pb[bank][:, ::-1]
                                if knew == 0:
                                    P.act(lambda e, fo=fo, fi=fi: e.activation(out=fo, in_=fi, func=AF.Copy), reads=[bres[bank]], writes=[r_ftB])
                                else:
                                    P.dve(lambda e, fo=fo, fi=fi: e.tensor_copy(out=fo, in_=fi), reads=[bres[bank]], writes=[r_ftB])
                G = load_G(T2, l, 0, b)
                for tt in range(nbt):
                    ti = offs[b] // TOK + tt
                    for j in range(NJ):
                        tok0 = tt * TOK + j * 128
                        for nh in range(2):
                            yb = nh
                            for k in range(KC):
                                P.pe(lambda e, k=k, tok0=tok0, nh=nh, yb=yb: e.matmul(
                                    pb[yb][:], lhsT=FT[:, k, tok0:tok0 + 128], rhs=Wfo[:, k, nh * 512:(nh + 1) * 512],
                                    start=(k == 0), stop=(k == KC - 1)), reads=[r_ft, r_ftB, r_wfo], writes=[bres[yb]])
                            resid_apply(T2, ti, j, nh, yb, G)
                        resid_store(T2, ti, j)
            P.barrier()

    def attn_phase(l, src, src_is_X):
        with contextlib.ExitStack() as ph:
            T = make_tools(ph, nxa=3, nxr=3, nhT=2, nxn=4)
            sbp = T.sbp
            NSC = SMAX // 128
            norder = []
            for b_ in range(NB):
                tl = list(range(offs[b_] // TOK, (offs[b_] + SEQS[b_]) // TOK))
                norder += tl + tl
            norm_setup(T, norder, src, src_is_X, l, 0, (0, 1))
            resid_setup(T, range(ntiles), src, src_is_X)
            Wqkv = sbp("Wqkv", [128, KC, 1536], BF16)
            Wao = sbp("Wao", [128, KC, D], BF16)
            KT = sbp("KT", [128, 2, SMAX], BF16)
            V = sbp("V", [128, NSC, 256], BF16)
            QT = sbp("QT", [128, 8, TOK], BF16)
            OT = sbp("OT", [128, 8, TOK], BF16)
            rmT = sbp("rmT", [128, 128], F32)
            gq = sbp("gq", [128, 2], F32)
            NR = 3
            cst = [Slot(sbp("cst%d" % i, [128, 2, TOK], F32), Res("cst"), "cst%d" % i) for i in range(2)]
            qraw = [Slot(sbp("qraw%d" % i, [128, TOK], F32), Res("qraw"), None) for i in range(NR)]
            qsq = [Slot(sbp("qsq%d" % i, [128, TOK], BF16), Res("qsq"), None) for i in range(NR)]
            qrs = [Slot(sbp("qrs%d" % i, [128, TOK], F32), Res("qrs"), None) for i in range(NR)]
            qn = [Slot(sbp("qn%d" % i, [128, TOK], F32), Res("qn"), None) for i in range(NR)]
            qt1 = [Slot(sbp("qt1%d" % i, [128, TOK], F32), Res("qt1"), None) for i in range(NR)]
            NPT = 6
            pT = [Slot(sbp("pT%d" % i, [128, TOK], BF16), Res("pT"), None) for i in range(NPT)]
            rden = [Slot(sbp("rden%d" % i, [128, TOK], F32), Res("rden"), None) for i in range(2)]
            r_w, r_kt, r_v, r_qt, r_ot, r_g = Res("wqkv"), Res("kt"), Res("v"), Res("qt"), Res("ot"), Res("gq")
            load_w_cast(Wqkv, r_w, W["attn_w_qkv"][0], "wqkv", KC, 1536, 1)
            load_w_cast(Wao, r_w, W["attn_w_o"][0], "wqkv", KC, D, 1)
            P.dma("sp", "gq", lambda e: e.dma_start(out=rmT[:], in_=rmT_d), writes=[r_g])
            P.dma("sp", "gq", lambda e: e.dma_start(out=gq[:, 0:1], in_=W["attn_q_g"].rearrange("o p -> p o")), writes=[r_g])
            P.dma("sp", "gq", lambda e: e.dma_start(out=gq[:, 1:2], in_=W["attn_k_g"].rearrange("o p -> p o")), writes=[r_g])
            P.dve(lambda e: e.tensor_scalar(out=gq[:, 0:1], in0=gq[:, 0:1], scalar1=128.0 ** -0.5, scalar2=None, op0=ALU.mult),
                  reads=[r_g], writes=[r_g])
            cnt = {"h": 0, "cs": 0, "p": 0, "rd": 0, "u": 0}

            def load_cs(tloc):
                c = cst[cnt["cs"] % 2]
                cnt["cs"] += 1
                P.dma("sp", c.key, lambda e, c=c, tloc=tloc: e.dma_start(out=c.t[:, 0, :], in_=ropec_d[:, tloc:tloc + TOK]), writes=[c.res])
                P.dma("sp", c.key, lambda e, c=c, tloc=tloc: e.dma_start(out=c.t[:, 1, :], in_=ropes_d[:, tloc:tloc + TOK]), writes=[c.res])
                return c

            def head_pipeline(jobs, hs, c):
                n = len(jobs)
                base = cnt["h"]
                cnt["h"] += n

                def s1(i):
                    col0 = jobs[i][0]
                    r = (base + i) % NR
                    bk = 2 + (base + i) % 2
                    for k in range(KC):
                        P.pe(lambda e, k=k, col0=col0, bk=bk: e.matmul(pb[bk][:], lhsT=Wqkv[:, k, col0:col0 + 128], rhs=hs.t[:, k, :],
                                                                       start=(k == 0), stop=(k == KC - 1)), reads=[r_w, hs.res], writes=[bres[bk]])
                    P.act(lambda e, r=r, bk=bk: e.activation(out=qraw[r].t[:], in_=pb[bk][:], func=AF.Copy), reads=[bres[bk]], writes=[qraw[r].res])
                    P.act(lambda e, r=r, bk=bk: e.activation(out=qsq[r].t[:], in_=pb[bk][:], func=AF.Square), reads=[bres[bk]], writes=[qsq[r].res])

                def s2(i):
                    gcol = jobs[i][1]
                    r = (base + i) % NR
                    bk = 4 + (base + i) % 2
                    P.pe(lambda e, r=r, bk=bk: e.matmul(pb[bk][:], lhsT=onesb[:], rhs=qsq[r].t[:], start=True, stop=True),
                         reads=[r_const, qsq[r].res], writes=[bres[bk]])
                    P.act(lambda e, r=r, bk=bk: e.activation(out=qrs[r].t[:], in_=pb[bk][:], func=AF.Sqrt, bias=EPS, scale=1.0),
                          reads=[bres[bk]], writes=[qrs[r].res])
                    P.dve(lambda e, r=r: e.reciprocal(out=qrs[r].t[:], in_=qrs[r].t[:]), reads=[qrs[r].res], writes=[qrs[r].res])
                    P.dve(lambda e, r=r, gcol=gcol: e.scalar_tensor_tensor(out=qn[r].t[:], in0=qraw[r].t[:], scalar=gq[:, gcol:gcol + 1], in1=qrs[r].t[:],
                                                                           op0=ALU.mult, op1=ALU.mult), reads=[qraw[r].res, qrs[r].res, r_g], writes=[qn[r].res])

                def s3(i):
                    out_ap, out_res = jobs[i][2], jobs[i][3]
                    r = (base + i) % NR
                    bk = 6 + (base + i) % 2
                    P.pe(lambda e, r=r, bk=bk: e.matmul(pb[bk][:], lhsT=rmT[:], rhs=qn[r].t[:], start=True, stop=True),
                         reads=[r_g, qn[r].res], writes=[bres[bk]])
                    P.dve(lambda e, r=r, bk=bk: e.tensor_tensor(out=qt1[r].t[:], in0=pb[bk][:], in1=c.t[:, 1, :], op=ALU.mult),
                          reads=[bres[bk], c.res], writes=[qt1[r].res])
                    P.pool(lambda e, r=r: e.tensor_tensor(out=qn[r].t[:], in0=qn[r].t[:], in1=c.t[:, 0, :], op=ALU.mult),
                           reads=[qn[r].res, c.res], writes=[qn[r].res])
                    P.dve(lambda e, r=r, out_ap=out_ap: e.tensor_tensor(out=out_ap, in0=qn[r].t[:], in1=qt1[r].t[:], op=ALU.add),
                          reads=[qn[r].res, qt1[r].res], writes=[out_res])

                for step in range(n + 2):
                    if step < n:
                        s1(step)
                    if 0 <= step - 1 < n:
                        s2(step - 1)
                    if 0 <= step - 2 < n:
                        s3(step - 2)

            def head_stage_closures(jobs, hs, c, bk):
                out = []
                for (col0, gcol, out_ap, out_res) in jobs:
                    box = {}

                    def s1(col0=col0, box=box):
                        r = cnt["h"] % NR
                        cnt["h"] += 1
                        box["r"] = r
                        for k in range(KC):
                            P.pe(lambda e, k=k, col0=col0: e.matmul(pb[bk][:], lhsT=Wqkv[:, k, col0:col0 + 128], rhs=hs.t[:, k, :],
                                                                    start=(k == 0), stop=(k == KC - 1)), reads=[r_w, hs.res], writes=[bres[bk]])
                        P.dve(lambda e, r=r: e.tensor_copy(out=qraw[r].t[:], in_=pb[bk][:]), reads=[bres[bk]], writes=[qraw[r].res])
                        P.dve(lambda e, r=r: e.tensor_tensor(out=qsq[r].t[:], in0=pb[bk][:], in1=qraw[r].t[:], op=ALU.mult),
                              reads=[bres[bk], qraw[r].res], writes=[qsq[r].res])

                    def s2(gcol=gcol, box=box):
                        r = box["r"]
                        P.pe(lambda e, r=r: e.matmul(pb[bk][:], lhsT=onesb[:], rhs=qsq[r].t[:], start=True, stop=True),
                             reads=[r_const, qsq[r].res], writes=[bres[bk]])
                        P.act(lambda e, r=r: e.activation(out=qrs[r].t[:], in_=pb[bk][:], func=AF.Sqrt, bias=EPS, scale=1.0),
                              reads=[bres[bk]], writes=[qrs[r].res])
                        P.dve(lambda e, r=r: e.reciprocal(out=qrs[r].t[:], in_=qrs[r].t[:]), reads=[qrs[r].res], writes=[qrs[r].res])
                        P.dve(lambda e, r=r, gcol=gcol: e.scalar_tensor_tensor(out=qn[r].t[:], in0=qraw[r].t[:], scalar=gq[:, gcol:gcol + 1], in1=qrs[r].t[:],
                                                                               op0=ALU.mult, op1=ALU.mult), reads=[qraw[r].res, qrs[r].res, r_g], writes=[qn[r].res])

                    def s3(out_ap=out_ap, out_res=out_res, box=box):
                        r = box["r"]
                        P.pe(lambda e, r=r: e.matmul(pb[bk][:], lhsT=rmT[:], rhs=qn[r].t[:], start=True, stop=True),
                             reads=[r_g, qn[r].res], writes=[bres[bk]])
                        P.dve(lambda e, r=r: e.tensor_tensor(out=qt1[r].t[:], in0=pb[bk][:], in1=c.t[:, 1, :], op=ALU.mult),
                              reads=[bres[bk], c.res], writes=[qt1[r].res])
                        P.pool(lambda e, r=r: e.tensor_tensor(out=qn[r].t[:], in0=qn[r].t[:], in1=c.t[:, 0, :], op=ALU.mult),
                               reads=[qn[r].res, c.res], writes=[qn[r].res])
                        P.pool(lambda e, r=r, out_ap=out_ap: e.tensor_tensor(out=out_ap, in0=qn[r].t[:], in1=qt1[r].t[:], op=ALU.add),
                               reads=[qn[r].res, qt1[r].res], writes=[out_res])
                    out += [s1, s2, s3]
                return out

            QTs = [Slot(QT, r_qt, None), Slot(sbp("QTb", [128, 8, TOK], BF16), Res("qtb"), None)]
            nq = 0
            cur = norm_next(T)
            for b in range(NB):
                S = SEQS[b]
                nsc = S // 128
                nbt = S // TOK
                t0i = offs[b] // TOK
                for tt in range(nbt):
                    c = load_cs(tt * TOK)
                    nxt, An, Bn = norm_next(T, split=True)
                    for f_ in An:
                        f_()
                    head_pipeline([(1024 + kv * 128, 1, KT[:, kv, tt * TOK:(tt + 1) * TOK], r_kt) for kv in range(2)], cur, c)
                    for j in range(NJ):
                        vb = 4 + (j % 2)
                        for k in range(KC):
                            P.pe(lambda e, k=k, j=j, vb=vb, cur=cur: e.matmul(pb[vb][:, 0:256], lhsT=cur.t[:, k, j * 128:(j + 1) * 128],
                                                                     rhs=Wqkv[:, k, 1280:1536], start=(k == 0), stop=(k == KC - 1)),
                                 reads=[r_w, cur.res], writes=[bres[vb]])
                        P.dve(lambda e, j=j, vb=vb, tt=tt: e.tensor_copy(out=V[:, tt * NJ + j, :], in_=pb[vb][:, 0:256]),
                              reads=[bres[vb]], writes=[r_v])
                    for f_ in Bn:
                        f_()
                    cur = nxt
                G = load_G(T, l, 0, b)
                units = [(h, sc) for h in range(8) for sc in range(nsc)]
                nu = len(units)
                LA = 3
                SB = (0, 1, 2, 3)
                SCR = 3
                OBDB = ((5, 6), (7, 4))
                qcur = QTs[nq % 2]
                nq += 1
                c = load_cs(0)
                head_pipeline([(h * 128, 0, qcur.t[:, h, :], qcur.res) for h in range(8)], cur, c)
                for tt in range(nbt):
                    ti = t0i + tt
                    inject = {}
                    qnext = None
                    inj = False
                    if tt + 1 < nbt and inj:
                        T.nparams = (l, 0, (SCR,))
                        hs_n, An, Bn = norm_next(T, split=True)
                        T.nparams = (l, 0, (0, 1))
                        qnext = QTs[nq % 2]
                        nq += 1
                        cn = load_cs((tt + 1) * TOK)
                        stages = head_stage_closures([(h * 128, 0, qnext.t[:, h, :], qnext.res) for h in range(8)], hs_n, cn, SCR)
                        evs = An + Bn + stages
                        step = max(1, (nu - 8) // len(evs))
                        for i_, f_ in enumerate(evs):
                            inject.setdefault(4 + i_ * step, []).append(f_)
                    pslot = {}

                    def qk(u, qcur=qcur):
                        h, sc = units[u]
                        kv = h // 4
                        sb_ = SB[(cnt["u"] + u) % len(SB)]
                        P.pe(lambda e, sc=sc, sb_=sb_, kv=kv, h=h, qcur=qcur: e.matmul(pb[sb_][:], lhsT=KT[:, kv, sc * 128:(sc + 1) * 128], rhs=qcur.t[:, h, :],
                                                                             start=True, stop=True), reads=[r_kt, qcur.res], writes=[bres[sb_]])
                        p_ = pT[cnt["p"] % NPT]
                        cnt["p"] += 1
                        P.act(lambda e, p_=p_, sb_=sb_: e.activation(out=p_.t[:], in_=pb[sb_][:], func=AF.Exp),
                              reads=[bres[sb_]], writes=[p_.res])
                        pslot[u] = p_

                    def pv(u, nsc=nsc):
                        h, sc = units[u]
                        kv = h // 4
                        ob, db = OBDB[h % 2]
                        p_ = pslot.pop(u)
                        P.pe(lambda e, sc=sc, p_=p_, kv=kv, nsc=nsc, ob=ob: e.matmul(pb[ob][:], lhsT=V[:, sc, kv * 128:(kv + 1) * 128], rhs=p_.t[:],
                                                                              start=(sc == 0), stop=(sc == nsc - 1)), reads=[r_v, p_.res], writes=[bres[ob]])
                        P.pe(lambda e, sc=sc, p_=p_, nsc=nsc, db=db: e.matmul(pb[db][:], lhsT=onesb[:], rhs=p_.t[:],
                                                                       start=(sc == 0), stop=(sc == nsc - 1)), reads=[r_const, p_.res], writes=[bres[db]])
                        if sc == nsc - 1:
                            rd = rden[cnt["rd"] % 2]
                            cnt["rd"] += 1
                            P.dve(lambda e, rd=rd, db=db: e.reciprocal(out=rd.t[:], in_=pb[db][:]), reads=[bres[db]], writes=[rd.res])
                            P.dve(lambda e, rd=rd, h=h, ob=ob: e.scalar_tensor_tensor(out=OT[:, h, :], in0=pb[ob][:], scalar=1.0 / 128.0, in1=rd.t[:],
                                                                                      op0=ALU.mult, op1=ALU.mult), reads=[bres[ob], rd.res], writes=[r_ot])

                    for u in range(min(LA, nu)):
                        qk(u)
                    for u in range(nu):
                        if u + LA < nu:
                            qk(u + LA)
                        pv(u)
                        for f_ in inject.pop(u, []):
                            f_()
                    for k_ in sorted(inject):
                        for f_ in inject[k_]:
                            f_()
                    cnt["u"] += nu
                    if tt + 1 < nbt and not inj:
                        cur = norm_next(T)
                        qnext = QTs[nq % 2]
                        nq += 1
                        cn = load_cs((tt + 1) * TOK)
                        head_pipeline([(h * 128, 0, qnext.t[:, h, :], qnext.res) for h in range(8)], cur, cn)
                    if tt + 1 == nbt:
                        cur = norm_next(T)
                    for j in range(NJ):
                        for nh in range(2):
                            yb = (3, 5)[nh]
                            for h in range(8):
                                P.pe(lambda e, h=h, j=j, nh=nh, yb=yb: e.matmul(
                                    pb[yb][:], lhsT=OT[:, h, j * 128:(j + 1) * 128], rhs=Wao[:, h, nh * 512:(nh + 1) * 512],
                                    start=(h == 0), stop=(h == 7)), reads=[r_ot, r_w], writes=[bres[yb]])
                            resid_apply(T, ti, j, nh, yb, G)
                        resid_store(T, ti, j)
                    qcur = qnext
            P.barrier()

    def sgu_phase(l, src, src_is_X):
        with contextlib.ExitStack() as ph:
            T = make_tools(ph, nxa=4, nxr=3, nhT=2, nxn=4)
            sbp = T.sbp
            Win = sbp("Win", [128, KC, 2 * D], BF16)
            Wso = sbp("Wso", [128, KC, D], BF16)
            WsT = sbp("WsT", [128, 8, 128], BF16)
            wstage = sbp("wstage", [128, 8, 128], F32)
            BS = sbp("BS", [128, 8, 128], F32)
            LNG = sbp("LNG", [128, D], F32)
            LNB = sbp("LNB", [128, D], F32)
            uT = sbp("uT", [128, KC, TOK], F32)
            aT = sbp("aT", [128, KC, TOK], BF16)
            NV = 4
            vg = [Slot(sbp("vg%d" % i, [128, D], F32), Res("vg"), None) for i in range(NV)]
            vnb = [Slot(sbp("vnb%d" % i, [128, D], BF16), Res("vnb"), None) for i in range(NV)]
            st6 = [Slot(sbp("st6%d" % i, [128, 2, 6], F32), Res("st6"), None) for i in range(NV)]
            mv = [Slot(sbp("mv%d" % i, [128, 2], F32), Res("mv"), None) for i in range(NV)]
            sd = [Slot(sbp("sd%d" % i, [128, 1], F32), Res("sd"), None) for i in range(NV)]
            svt = [Slot(sbp("svt%d" % i, [128, 4, 128], F32), Res("svt"), None) for i in range(4)]
            r_w, r_c, r_u, r_a = Res("w"), Res("c"), Res("u"), Res("a")
            load_w_cast(Win, r_w, W["sgu_w_in"][0], "wsgu", KC, 2 * D, 1)
            load_w_cast(Wso, r_w, W["sgu_w_o"][0], "wsgu", KC, D, 1)
            P.dma("sp", "sguc", lambda e: e.dma_start(out=wstage[:], in_=W["sgu_w_s"][0].rearrange("g i j -> i g j")), writes=[r_c])
            P.dma("sp", "sguc", lambda e: e.dma_start(out=BS[:].rearrange("p g i -> p (g i)"),
                                                     in_=W["sgu_b_s"].rearrange("o g i -> o (g i)").partition_broadcast(128)), writes=[r_c])
            P.dma("sp", "sguc", lambda e: e.dma_start(out=LNG[:], in_=W["sgu_ln_g"].partition_broadcast(128)), writes=[r_c])
            P.dma("sp", "sguc", lambda e: e.dma_start(out=LNB[:], in_=W["sgu_ln_b"].partition_broadcast(128)), writes=[r_c])
            for g in range(8):
                bank = 2 + (g % 2)
                P.pe(lambda e, g=g, bank=bank: e.transpose(out=pb[bank][:, 0:128], in_=wstage[:, g, :], identity=identf[:]),
                     reads=[r_c, r_const], writes=[bres[bank]])
                P.act(lambda e, g=g, bank=bank: e.activation(out=WsT[:, g, :], in_=pb[bank][:, 0:128], func=AF.Copy),
                      reads=[bres[bank]], writes=[r_c])
            G = None
            gb_ = -1
            cntv = {"nv": 0, "nsv": 0}
            norm_setup(T, range(ntiles), src, src_is_X, l, 0, (0, 1))
            resid_setup(T, range(ntiles), src, src_is_X)

            def emit_v(hs):
                vi = []
                for j in range(NJ):
                    i = cntv["nv"] % NV
                    cntv["nv"] += 1
                    vi.append(i)
                    for nh in range(2):
                        bank = 4 + 2 * (j % 2) + nh
                        for k in range(KC):
                            P.pe(lambda e, k=k, j=j, nh=nh, bank=bank: e.matmul(
                                pb[bank][:], lhsT=hs.t[:, k, j * 128:(j + 1) * 128], rhs=Win[:, k, D + nh * 512:D + (nh + 1) * 512],
                                start=(k == 0), stop=(k == KC - 1)), reads=[r_w, hs.res], writes=[bres[bank]])
                        P.act(lambda e, nh=nh, bank=bank, i=i: e.activation(out=vg[i].t[:, nh * 512:(nh + 1) * 512], in_=pb[bank][:], func=AF.Gelu),
                              reads=[bres[bank]], writes=[vg[i].res])
                    for nh in range(2):
                        P.dve(lambda e, nh=nh, i=i: e.bn_stats(out=st6[i].t[:, nh, :], in_=vg[i].t[:, nh * 512:(nh + 1) * 512]),
                              reads=[vg[i].res], writes=[st6[i].res])
                    P.dve(lambda e, i=i: e.bn_aggr(out=mv[i].t[:], in_=st6[i].t[:].rearrange("p a b -> p (a b)")),
                          reads=[st6[i].res], writes=[mv[i].res])
                    P.act(lambda e, i=i: e.activation(out=sd[i].t[:], in_=mv[i].t[:, 1:2], func=AF.Sqrt, bias=EPS, scale=1.0),
                          reads=[mv[i].res], writes=[sd[i].res])
                    P.dve(lambda e, i=i: e.reciprocal(out=sd[i].t[:], in_=sd[i].t[:]), reads=[sd[i].res], writes=[sd[i].res])
                    P.dve(lambda e, i=i: e.tensor_scalar(out=vg[i].t[:], in0=vg[i].t[:], scalar1=mv[i].t[:, 0:1], scalar2=sd[i].t[:, 0:1],
                                                         op0=ALU.subtract, op1=ALU.mult), reads=[vg[i].res, mv[i].res, sd[i].res], writes=[vg[i].res])
                    P.pool(lambda e, i=i: e.tensor_tensor(out=vg[i].t[:], in0=vg[i].t[:], in1=LNG[:], op=ALU.mult),
                           reads=[vg[i].res, r_c], writes=[vg[i].res])
                    P.pool(lambda e, i=i: e.tensor_tensor(out=vnb[i].t[:], in0=vg[i].t[:], in1=LNB[:], op=ALU.add),
                           reads=[vg[i].res, r_c], writes=[vnb[i].res])
                return vi

            cur = norm_next(T)
            vi_cur = emit_v(cur)
            for ti in range(ntiles):
                b = seq_of_tile(ti)
                if b != gb_:
                    G = load_G(T, l, 0, b)
                    gb_ = b
                nxt, An, Bn = norm_next(T, split=True)
                for uc in range(KC):
                    if An and uc in (1, 3, 5, 7):
                        An[(uc - 1) // 2]()
                    bank = 2 + (uc % 2)
                    for k in range(KC):
                        P.pe(lambda e, k=k, uc=uc, bank=bank, cur=cur: e.matmul(pb[bank][:], lhsT=Win[:, k, uc * 128:(uc + 1) * 128], rhs=cur.t[:, k, :],
                                                                        start=(k == 0), stop=(k == KC - 1)), reads=[r_w, cur.res], writes=[bres[bank]])
                    P.act(lambda e, uc=uc, bank=bank: e.activation(out=uT[:, uc, :], in_=pb[bank][:], func=AF.Gelu),
                          reads=[bres[bank]], writes=[r_u])
                for f_ in Bn:
                    f_()
                for j in range(NJ):
                    i = vi_cur[j]
                    for gh in range(2):
                        bank = 2 + gh
                        for g4 in range(4):
                            g = gh * 4 + g4
                            P.pe(lambda e, g=g, g4=g4, i=i, bank=bank: e.matmul(pb[bank][:, g4 * 128:(g4 + 1) * 128], lhsT=vnb[i].t[:, g * 128:(g + 1) * 128],
                                                                                 rhs=WsT[:, g, :], start=True, stop=True),
                                 reads=[vnb[i].res, r_c], writes=[bres[bank]])
                        sv = svt[cntv["nsv"] % 4]
                        cntv["nsv"] += 1
                        P.dve(lambda e, sv=sv, gh=gh, bank=bank: e.tensor_tensor(out=sv.t[:], in0=pb[bank][:].rearrange("p (g i) -> p g i", i=128),
                                                                                 in1=BS[:, gh * 4:(gh + 1) * 4, :], op=ALU.add),
                              reads=[bres[bank], r_c], writes=[sv.res])
                        P.pool(lambda e, sv=sv, gh=gh, j=j: e.tensor_tensor(out=aT[:, gh * 4:(gh + 1) * 4, j * 128:(j + 1) * 128], in0=sv.t[:],
                                                                            in1=uT[:, gh * 4:(gh + 1) * 4, j * 128:(j + 1) * 128], op=ALU.mult),
                               reads=[sv.res, r_u], writes=[r_a])
                vi_next = emit_v(nxt) if nxt is not None else None
                for j in range(NJ):
                    for nh in range(2):
                        yb = 2 + nh
                        for g in range(8):
                            P.pe(lambda e, g=g, j=j, nh=nh, yb=yb: e.matmul(pb[yb][:], lhsT=aT[:, g, j * 128:(j + 1) * 128],
                                                                             rhs=Wso[:, g, nh * 512:(nh + 1) * 512], start=(g == 0), stop=(g == 7)),
                                 reads=[r_a, r_w], writes=[bres[yb]])
                        resid_apply(T, ti, j, nh, yb, G)
                    resid_store(T, ti, j)
                cur = nxt
                vi_cur = vi_next
            P.barrier()

    def final_phase(src, src_is_X):
        with contextlib.ExitStack() as ph:
            def sbp(name, shape, dt):
                uid[0] += 1
                return ph.enter_context(nc.sbuf_tensor("%s_%d" % (name, uid[0]), list(shape), dt))
            FG = sbp("FG", [128, D], F32)
            r_fg = Res("fg")
            xi = [Slot(sbp("fxi%d" % i, [128, D], F32), Res("fxi"), "fxi%d" % i) for i in range(4)]
            yo = [Slot(sbp("fyo%d" % i, [128, D], F32), Res("fyo"), "fyo%d" % i) for i in range(3)]
            junk = sbp("fjunk", [128, D], BF16)
            r_junk = Res("junk")
            P.dma("sp", "fg", lambda e: e.dma_start(out=FG[:], in_=W["final_g"].rearrange("(o d) -> o d", o=1).partition_broadcast(128)), writes=[r_fg])
            for g in range(NSUB):
                s = xi[g % 4]
                o = yo[g % 3]
                ss, sq, rs = ssr.next(), sqr.next(), rsr.next()
                P.dma("sp", s.key, lambda e, s=s, g=g: e.dma_start(out=s.t[:], in_=src[g * 128:(g + 1) * 128, :]),
                      reads=[XR[g]] if src_is_X else [], writes=[s.res])
                P.act(lambda e, s=s, ss=ss: e.activation(out=junk[:], in_=s.t[:], func=AF.Square, accum_out=ss.t[:]),
                      reads=[s.res], writes=[r_junk, ss.res], noatt=True)
                P.act(lambda e, ss=ss, sq=sq: e.activation(out=sq.t[:], in_=ss.t[:], func=AF.Sqrt, bias=EPS, scale=1.0 / D),
                      reads=[ss.res], writes=[sq.res])
                P.dve(lambda e, sq=sq, rs=rs: e.reciprocal(out=rs.t[:], in_=sq.t[:]), reads=[sq.res], writes=[rs.res])
                P.dve(lambda e, s=s, o=o, rs=rs: e.scalar_tensor_tensor(out=o.t[:], in0=s.t[:], scalar=rs.t[:, 0:1], in1=FG[:],
                                                                        op0=ALU.mult, op1=ALU.mult), reads=[s.res, rs.res, r_fg], writes=[o.res])
                P.dma("pool", "st_" + o.key, lambda e, o=o, g=g: e.dma_start(out=yout[g * 128:(g + 1) * 128, :], in_=o.t[:]), reads=[o.res])

    first = True
    for l in layers:
        src, isx = (xin, False) if first else (X, True)
        first = False
        kind = l % 3
        if kind == 0:
            fnet_phase(l, src, isx)
        elif kind == 1:
            attn_phase(l, src, isx)
        else:
            sgu_phase(l, src, isx)
        ffn_phase(l)
    final_phase(X if not first else xin, not first)
    P.emit(nc)
    return nc, P


_CACHE = {}


def kernel(x_prompt, x_sample, c_prompt, c_sample, **w):
    ncores = 8
    Bp, Sp, _ = x_prompt.shape
    Bs, Ss, _ = x_sample.shape
    npc, nsc = Bp // ncores, Bs // ncores
    SEQS = tuple([Sp] * npc + [Ss] * nsc)
    key = SEQS
    if key not in _CACHE:
        _CACHE[key] = (build(SEQS)[0], _consts(SEQS))
    nc, consts = _CACHE[key]
    in_maps = []
    for c in range(ncores):
        xp = np.asarray(x_prompt[c * npc:(c + 1) * npc], np.float32).reshape(npc * Sp, D)
        xs = np.asarray(x_sample[c * nsc:(c + 1) * nsc], np.float32).reshape(nsc * Ss, D)
        m = {"xin": np.ascontiguousarray(np.concatenate([xp, xs], 0)),
             "cvec": np.ascontiguousarray(np.concatenate([np.asarray(c_prompt[c * npc:(c + 1) * npc], np.float32),
                                                          np.asarray(c_sample[c * nsc:(c + 1) * nsc], np.float32)], 0))}
        for n, _s in WSPECS:
            m[n] = np.ascontiguousarray(np.asarray(w[n], np.float32))
        m.update(consts)
        in_maps.append(m)
    res = run_bass_kernel_spmd(nc, in_maps, core_ids=list(range(ncores)))
    yp = np.empty((Bp, Sp, D), np.float32)
    ys = np.empty((Bs, Ss, D), np.float32)
    for c in range(ncores):
        y = np.asarray(res.results[c]["yout"])
        yp[c * npc:(c + 1) * npc] = y[:npc * Sp].reshape(npc, Sp, D)
        ys[c * nsc:(c + 1) * nsc] = y[npc * Sp:].reshape(nsc, Ss, D)
    return (yp, ys)
```
